# Optimizing a Trainium2 kernel written in Bass

```python
import math
import jax, jax.numpy as jnp
from jax import lax
import numpy as np

D_MODEL = 2048
BATCH = 16
SEQ = 2048
DEPTH = 4

CHUNK = 64
Q_BLOCK = 128
N_A_LAYERS = DEPTH // 2
N_B_LAYERS = DEPTH - N_A_LAYERS
HEAD_DIM = 128
A_HEADS = D_MODEL // (2 * HEAD_DIM)
A_QK_WIDTH = A_HEADS * 2 * HEAD_DIM
A_V_WIDTH = A_HEADS * 2 * HEAD_DIM
B_HEADS = D_MODEL // HEAD_DIM
B_WIDTH = B_HEADS * HEAD_DIM
ROT_DIM = HEAD_DIM // 4
ROPE_THETA = 500000.0
D_FF = 256 * ((8 * D_MODEL // 3 + 255) // 256)
CONV_WIDTH = 3
EPS = 1e-6

kernel_name = 'yoco_diffattn_stickbreaking_convffn'


def _rmsnorm(x, g):
    xf = x.astype(jnp.float32)
    y = xf * lax.rsqrt(jnp.mean(xf * xf, axis=-1, keepdims=True) + EPS)
    return (y * g.astype(jnp.float32)).astype(x.dtype)


def _rotary_tables(positions):
    inv_freq = ROPE_THETA ** (-jnp.arange(0, ROT_DIM, 2, dtype=jnp.float32) / ROT_DIM)
    ang = positions.astype(jnp.float32)[..., None] * inv_freq
    return jnp.cos(ang), jnp.sin(ang)


def _apply_partial_rotary(x, cos, sin):
    cos = cos[:, :, None, None, :]
    sin = sin[:, :, None, None, :]
    half = ROT_DIM // 2
    xr = x[..., :ROT_DIM].astype(jnp.float32)
    x1, x2 = xr[..., :half], xr[..., half:]
    rot = jnp.concatenate([x1 * cos - x2 * sin, x2 * cos + x1 * sin], axis=-1).astype(x.dtype)
    return jnp.concatenate([rot, x[..., ROT_DIM:]], axis=-1)


def _diff_attention(xn, w_qkv, q_g, k_g, lq1, lk1, lq2, lk2, subln_g, w_o, cos, sin, lambda_init):
    b, s, _ = xn.shape
    qkv = xn @ w_qkv
    q = qkv[..., :A_QK_WIDTH].reshape(b, s, A_HEADS, 2, HEAD_DIM)
    k = qkv[..., A_QK_WIDTH:2 * A_QK_WIDTH].reshape(b, s, A_HEADS, 2, HEAD_DIM)
    v = qkv[..., 2 * A_QK_WIDTH:].reshape(b, s, A_HEADS, 2 * HEAD_DIM)
    q = _apply_partial_rotary(_rmsnorm(q, q_g), cos, sin)
    k = _apply_partial_rotary(_rmsnorm(k, k_g), cos, sin)
    f32 = jnp.float32
    lam = (jnp.exp(jnp.sum(lq1.astype(f32) * lk1.astype(f32)))
           - jnp.exp(jnp.sum(lq2.astype(f32) * lk2.astype(f32))) + lambda_init)
    scale = HEAD_DIM ** -0.5
    outs = []
    for i0 in range(0, s, Q_BLOCK):
        kend = i0 + Q_BLOCK
        sc = jnp.einsum('bqhcd,bkhcd->bhcqk', q[:, i0:kend], k[:, :kend],
                        preferred_element_type=f32) * scale
        q_chunk = (i0 + jnp.arange(Q_BLOCK)) // CHUNK
        k_chunk = jnp.arange(kend) // CHUNK
        mask = k_chunk[None, :] <= q_chunk[:, None]
        p = jax.nn.softmax(jnp.where(mask, sc, -jnp.inf), axis=-1)
        w = p[:, :, 0] - lam * p[:, :, 1]
        outs.append(jnp.einsum('bhqk,bkhe->bqhe', w.astype(v.dtype), v[:, :kend]))
    o = jnp.concatenate(outs, axis=1)
    o = _rmsnorm(o, subln_g) * (1.0 - lambda_init)
    return o.reshape(b, s, A_V_WIDTH) @ w_o


def _stick_breaking(xn, w_q, k, v, w_o):
    b, s, _ = xn.shape
    q = (xn @ w_q).reshape(b, s, B_HEADS, HEAD_DIM)
    scale = HEAD_DIM ** -0.5
    outs = []
    for i0 in range(0, s, Q_BLOCK):
        kend = i0 + Q_BLOCK
        z = jnp.einsum('bqhd,bkhd->bhqk', q[:, i0:kend], k[:, :kend],
                       preferred_element_type=jnp.float32) * scale
        q_pos = i0 + jnp.arange(Q_BLOCK)
        k_pos = jnp.arange(kend)
        mask = k_pos[None, :] < q_pos[:, None]
        log_keep = jnp.where(mask, -jax.nn.softplus(z), 0.0)
        between = lax.cumsum(log_keep, axis=3, reverse=True) - log_keep
        log_a = jax.nn.log_sigmoid(z) + between
        a = jnp.where(mask, jnp.exp(log_a), 0.0)
        outs.append(jnp.einsum('bhqk,bkhd->bqhd', a.astype(v.dtype), v[:, :kend]))
    o = jnp.concatenate(outs, axis=1)
    return o.reshape(b, s, B_WIDTH) @ w_o


def _conv_ffn(xn, w_up, conv_w, conv_b, w_down):
    up = xn @ w_up
    u, g = up[..., :D_FF], up[..., D_FF:]
    g = lax.conv_general_dilated(g, conv_w[:, None, :], window_strides=(1,),
                                 padding=[(CONV_WIDTH - 1, 0)],
                                 dimension_numbers=('NWC', 'WIO', 'NWC'),
                                 feature_group_count=D_FF) + conv_b
    return (jax.nn.silu(g) * u) @ w_down


def setup_inputs(seed: int = 0) -> dict:
    key = jax.random.key(seed)
    ks = jax.random.split(key, 24)
    f32 = jnp.float32

    def nrm(k, shape, scale):
        return jax.random.normal(k, shape, f32) * scale

    def gain(k, shape):
        return 1.0 + 0.02 * jax.random.normal(k, shape, f32)

    x = jax.random.normal(ks[0], (BATCH, SEQ, D_MODEL), f32)
    offsets = jax.random.randint(ks[1], (BATCH, 1), 0, 64) * CHUNK
    positions = (offsets + jnp.arange(SEQ, dtype=jnp.int32)[None, :]).astype(jnp.int32)
    return {
        'x': x,
        'positions': positions,
        'attn_norm_g': gain(ks[2], (DEPTH, D_MODEL)),
        'ffn_norm_g': gain(ks[3], (DEPTH, D_MODEL)),
        'a_w_qkv': nrm(ks[4], (N_A_LAYERS, D_MODEL, 2 * A_QK_WIDTH + A_V_WIDTH), D_MODEL ** -0.5),
        'a_q_norm_g': gain(ks[5], (N_A_LAYERS, HEAD_DIM)),
        'a_k_norm_g': gain(ks[6], (N_A_LAYERS, HEAD_DIM)),
        'a_lambda_q1': nrm(ks[7], (N_A_LAYERS, HEAD_DIM), 0.1),
        'a_lambda_k1': nrm(ks[8], (N_A_LAYERS, HEAD_DIM), 0.1),
        'a_lambda_q2': nrm(ks[9], (N_A_LAYERS, HEAD_DIM), 0.1),
        'a_lambda_k2': nrm(ks[10], (N_A_LAYERS, HEAD_DIM), 0.1),
        'a_subln_g': gain(ks[11], (N_A_LAYERS, 2 * HEAD_DIM)),
        'a_w_o': nrm(ks[12], (N_A_LAYERS, A_V_WIDTH, D_MODEL), A_V_WIDTH ** -0.5),
        'kv_norm_g': gain(ks[13], (D_MODEL,)),
        'b_w_kv': nrm(ks[14], (D_MODEL, 2 * B_WIDTH), D_MODEL ** -0.5),
        'b_w_q': nrm(ks[15], (N_B_LAYERS, D_MODEL, B_WIDTH), D_MODEL ** -0.5),
        'b_w_o': nrm(ks[16], (N_B_LAYERS, B_WIDTH, D_MODEL), B_WIDTH ** -0.5),
        'ffn_w_up': nrm(ks[17], (DEPTH, D_MODEL, 2 * D_FF), D_MODEL ** -0.5),
        'ffn_conv_w': nrm(ks[18], (DEPTH, CONV_WIDTH, D_FF), CONV_WIDTH ** -0.5),
        'ffn_conv_b': nrm(ks[19], (DEPTH, D_FF), 0.01),
        'ffn_w_down': nrm(ks[20], (DEPTH, D_FF, D_MODEL), D_FF ** -0.5),
    }


def reference(x, positions, attn_norm_g, ffn_norm_g, a_w_qkv, a_q_norm_g, a_k_norm_g,
              a_lambda_q1, a_lambda_k1, a_lambda_q2, a_lambda_k2, a_subln_g, a_w_o,
              kv_norm_g, b_w_kv, b_w_q, b_w_o, ffn_w_up, ffn_conv_w, ffn_conv_b, ffn_w_down):
    cos, sin = _rotary_tables(positions)
    b, s, _ = x.shape
    h = x
    k_shared = None
    v_shared = None
    for layer in range(DEPTH):
        if layer < N_A_LAYERS:
            lambda_init = 0.8 - 0.6 * math.exp(-0.3 * layer)
            xn = _rmsnorm(h, attn_norm_g[layer])
            h = h + _diff_attention(xn, a_w_qkv[layer], a_q_norm_g[layer], a_k_norm_g[layer],
                                    a_lambda_q1[layer], a_lambda_k1[layer],
                                    a_lambda_q2[layer], a_lambda_k2[layer],
                                    a_subln_g[layer], a_w_o[layer], cos, sin, lambda_init)
        else:
            if layer == N_A_LAYERS:
                kv = _rmsnorm(h, kv_norm_g) @ b_w_kv
                k_shared = kv[..., :B_WIDTH].reshape(b, s, B_HEADS, HEAD_DIM)
                v_shared = kv[..., B_WIDTH:].reshape(b, s, B_HEADS, HEAD_DIM)
            j = layer - N_A_LAYERS
            xn = _rmsnorm(h, attn_norm_g[layer])
            h = h + _stick_breaking(xn, b_w_q[j], k_shared, v_shared, b_w_o[j])
        h = h + _conv_ffn(_rmsnorm(h, ffn_norm_g[layer]), ffn_w_up[layer], ffn_conv_w[layer],
                          ffn_conv_b[layer], ffn_w_down[layer])
    return h
```

```python
import math
from contextlib import ExitStack

import numpy as np
import concourse.bass as bass
import concourse.mybir as mybir
from concourse.bass_utils import run_bass_kernel_spmd

F32 = mybir.dt.float32
BF16 = mybir.dt.bfloat16
I32 = mybir.dt.int32
AF = mybir.ActivationFunctionType
ALU = mybir.AluOpType
AX = mybir.AxisListType

D = 2048
NCH = 16
DFF = 5632
NFF = 44
TT = 512
EPS = 1e-6
SCALE = 128 ** -0.5
SEM_EPOCH = 30000


class Op:
    __slots__ = ("eng", "fn", "deps", "ndma", "key", "sig", "awaited", "idx", "join")


class Sched:
    def __init__(self):
        self.ops = []
        self.last_w = {}
        self.readers = {}
        self.last_op = {}

    def barrier(self):
        deps = sorted(set(self.last_op.values()))
        for e in ("pe", "act", "dve", "pool", "sp"):
            idx = self.add(e, lambda eh: None, join=True)
            op = self.ops[idx]
            op.deps = list(deps)
            for d in deps:
                self.ops[d].awaited = True
        self.last_w = {}
        self.readers = {}

    def add(self, eng, fn, reads=(), writes=(), ndma=0, key=None, join=False):
        idx = len(self.ops)
        deps = set()
        for r in reads:
            w = self.last_w.get(r)
            if w is not None:
                deps.add(w)
        for r in writes:
            w = self.last_w.get(r)
            if w is not None:
                deps.add(w)
            for rd in self.readers.get(r, ()):
                deps.add(rd)
        for r in reads:
            self.readers.setdefault(r, []).append(idx)
        for r in writes:
            self.last_w[r] = idx
            self.readers[r] = []
        op = Op()
        op.eng, op.fn, op.ndma, op.key, op.idx = eng, fn, ndma, key, idx
        op.awaited = False
        op.sig = None
        op.join = join
        if not join:
            self.last_op[("d", key) if ndma else ("e", eng)] = idx
        if ndma == 0 and eng == "pe":
            deps = {d for d in deps if not (self.ops[d].eng == "pe" and self.ops[d].ndma == 0)}
        deps.discard(idx)
        op.deps = sorted(deps)
        for d in op.deps:
            self.ops[d].awaited = True
        self.ops.append(op)
        return idx

    def emit(self, nc, stack):
        engs = ["pe", "act", "dve", "pool", "sp"]
        sems = {}

        def get_sem(name):
            if name not in sems:
                sems[name] = stack.enter_context(nc.semaphore(name))
            return sems[name]

        cnt = {e: [0, 0] for e in engs}
        dcnt = {}
        for op in self.ops:
            if op.ndma:
                st = dcnt.setdefault(op.key, [0, 0])
                inc = 16 * op.ndma
                if st[1] + inc > SEM_EPOCH:
                    st[0] += 1
                    st[1] = 0
                st[1] += inc
                op.sig = ("d_%s_%d" % (op.key, st[0]), st[1])
            elif op.awaited:
                st = cnt[op.eng]
                if st[1] + 1 > SEM_EPOCH:
                    st[0] += 1
                    st[1] = 0
                st[1] += 1
                op.sig = ("e_%s_%d" % (op.eng, st[0]), st[1])
        for op in self.ops:
            if op.sig is not None:
                get_sem(op.sig[0])
        self.nsems = len(sems)
        per_eng = {e: [op for op in self.ops if op.eng == e] for e in engs}
        ops = self.ops

        def run_engine(ename, eh):
            waited = {}
            for op in per_eng[ename]:
                need = {}
                for d in op.deps:
                    s = ops[d].sig
                    if need.get(s[0], 0) < s[1]:
                        need[s[0]] = s[1]
                for sname, v in need.items():
                    if waited.get(sname, 0) >= v:
                        continue
                    eh.wait_ge(sems[sname], v)
                    waited[sname] = v
                r = op.fn(eh)
                if r is None:
                    assert op.sig is None
                    continue
                if op.ndma:
                    sem = sems[op.sig[0]]
                    assert len(r) == op.ndma
                    for ins in r:
                        ins.then_inc(sem, 16)
                elif op.sig is not None:
                    r.then_inc(sems[op.sig[0]], 1)

        with nc.Block() as block:
            @block.tensor
            def _(e):
                run_engine("pe", e)

            @block.scalar
            def _(e):
                run_engine("act", e)

            @block.vector
            def _(e):
                run_engine("dve", e)

            @block.gpsimd
            def _(e):
                run_engine("pool", e)

            @block.sync
            def _(e):
                run_engine("sp", e)


C_ID, C_ONES, C_NTRI, C_NONES, C_MASKB, C_RROT, C_INVF, NCST = 0, 128, 256, 384, 512, 640, 672, 680
P_ATTN, P_FFN, P_KV, P_CW, P_CB, P_QG, P_KG, P_LAM, NPROW = 0, 64, 128, 144, 672, 848, 850, 852, 896


def make_consts():
    c = np.zeros((128, NCST), np.float32)
    c[:, C_ID:C_ID + 128] = np.eye(128, dtype=np.float32)
    c[:, C_ONES:C_ONES + 128] = 1.0
    kk = np.arange(128)
    c[:, C_NTRI:C_NTRI + 128] = -(kk[:, None] >= kk[None, :]).astype(np.float32)
    c[:, C_NONES:C_NONES + 128] = -1.0
    c[:, C_MASKB:C_MASKB + 128] = (kk[None, :] > kk[:, None]).astype(np.float32)
    r = np.zeros((128, 32), np.float32)
    for d in range(16):
        r[d + 16, d] = -1.0
        r[d, d + 16] = 1.0
    c[:, C_RROT:C_RROT + 32] = r
    inv = (np.float32(500000.0) ** (-np.arange(0, 32, 2, dtype=np.float32) / np.float32(32))).astype(np.float32)
    c[0:16, C_INVF] = inv
    c[16:32, C_INVF] = inv
    return c


def pack_params(attn_norm_g, ffn_norm_g, kv_norm_g, ffn_conv_w, ffn_conv_b, a_q_norm_g, a_k_norm_g,
                lq1, lk1, lq2, lk2):
    p = np.zeros((NPROW, 128), np.float32)
    p[P_ATTN:P_ATTN + 64] = attn_norm_g.reshape(64, 128)
    p[P_FFN:P_FFN + 64] = ffn_norm_g.reshape(64, 128)
    p[P_KV:P_KV + 16] = kv_norm_g.reshape(16, 128)
    p[P_CW:P_CW + 528] = ffn_conv_w.reshape(4 * 3 * NFF, 128)
    p[P_CB:P_CB + 176] = ffn_conv_b.reshape(4 * NFF, 128)
    p[P_QG:P_QG + 2] = a_q_norm_g
    p[P_KG:P_KG + 2] = a_k_norm_g
    for l in range(2):
        p[P_LAM + 4 * l + 0] = lq1[l]
        p[P_LAM + 4 * l + 1] = lk1[l]
        p[P_LAM + 4 * l + 2] = lq2[l]
        p[P_LAM + 4 * l + 3] = lk2[l]
    return p


def build(S=2048, NSEQ=2, n_layers=4, dbg=False):
    nc = bass.Bass("TRN2", target_bir_lowering=False)
    NT = S // 128
    NTT = S // TT
    NQ = TT // 128
    SCH = Sched()
    kind_dbg = "ExternalOutput" if dbg else "Internal"

    x_d = nc.dram_tensor("x", [NSEQ, S, D], F32, kind="ExternalInput")
    pos_d = nc.dram_tensor("pos", [NSEQ, S], I32, kind="ExternalInput")
    cst_d = nc.dram_tensor("cst", [128, NCST], F32, kind="ExternalInput")
    prm_d = nc.dram_tensor("prm", [NPROW, 128], F32, kind="ExternalInput")
    sub_d = nc.dram_tensor("subln", [1, 512], F32, kind="ExternalInput")
    wqkv_d = nc.dram_tensor("a_w_qkv", [2, D, 3 * D], F32, kind="ExternalInput")
    awo_d = nc.dram_tensor("a_w_o", [2, D, D], F32, kind="ExternalInput")
    bkv_d = nc.dram_tensor("b_w_kv", [D, 2 * D], F32, kind="ExternalInput")
    bwq_d = nc.dram_tensor("b_w_q", [2, D, D], F32, kind="ExternalInput")
    bwo_d = nc.dram_tensor("b_w_o", [2, D, D], F32, kind="ExternalInput")
    wup_d = nc.dram_tensor("ffn_w_up", [4, D, 2 * DFF], F32, kind="ExternalInput")
    wdn_d = nc.dram_tensor("ffn_w_down", [4, DFF, D], F32, kind="ExternalInput")
    out_d = nc.dram_tensor("out", [NSEQ, S, D], F32, kind="ExternalOutput")

    hT_d = nc.dram_tensor("hT", [NSEQ, NCH, 128, S], F32, kind=kind_dbg)
    cs_d = nc.dram_tensor("cs", [NSEQ, 2, 32, S], F32, kind=kind_dbg)
    qT_d = nc.dram_tensor("qT", [NSEQ, NCH, 128, S], BF16, kind=kind_dbg)
    kT_d = nc.dram_tensor("kT", [NSEQ, NCH, 128, S], BF16, kind=kind_dbg)
    v_d = nc.dram_tensor("v", [NSEQ, S, D], BF16, kind=kind_dbg)
    oT_d = nc.dram_tensor("oT", [NSEQ, NCH, 128, S], BF16, kind=kind_dbg)

    def sb(name, shape, dt):
        return nc.alloc_sbuf_tensor(name, [128] + list(shape), dt)

    CST = sb("CST", [NCST], F32)
    CBF = sb("CBF", [NCST], BF16)
    PT = sb("PT", [NPROW], F32)
    GS = sb("GS", [144], F32)
    SG = sb("SG", [512], F32)
    LAM = sb("LAM", [16], F32)
    QKG = sb("QKG", [4], F32)
    HALO = sb("HALO", [NFF, 2], F32)
    ARENA_WORDS = 44000
    ARENA = sb("ARENA", [ARENA_WORDS], F32)
    PS = [nc.alloc_psum_tensor("ps%d" % i, [128, 512], F32) for i in range(8)]

    ident = CST[:, C_ID:C_ID + 128]
    ones_f = CST[:, C_ONES:C_ONES + 128]
    ones_b = CBF[:, C_ONES:C_ONES + 128]
    ntri_b = CBF[:, C_NTRI:C_NTRI + 128]
    nones_b = CBF[:, C_NONES:C_NONES + 128]
    maskb_b = CBF[:, C_MASKB:C_MASKB + 128]
    rrot_b = CBF[0:32, C_RROT:C_RROT + 32]

    class Arena:
        def __init__(self):
            self.off = 0

        def reset(self):
            self.off = 0

        def alloc(self, shape, dt):
            n = int(np.prod(shape))
            words = (n * (2 if dt == BF16 else 4) + 3) // 4
            words = (words + 7) // 8 * 8
            assert self.off + words <= ARENA_WORDS, ("arena overflow", self.off, words)
            v = ARENA[:, self.off:self.off + words]
            self.off += words
            if dt == BF16:
                v = v.bitcast(BF16)
            elif dt == I32:
                v = v.bitcast(I32)
            v = v[:, 0:n]
            if len(shape) == 2:
                v = v.rearrange("p (a b) -> p a b", a=shape[0])
            elif len(shape) == 3:
                v = v.rearrange("p (a b c) -> p a b c", a=shape[0], b=shape[1])
            return v

    AR = Arena()

    def dma(q, pairs, reads, writes, key):
        def fn(e, pairs=pairs):
            return [e.dma_start(out=d, in_=s) for d, s in pairs]
        SCH.add(q, fn, reads, writes, ndma=len(pairs), key=key)

    def mm(ps, pairs, reads, writes, start=True, stop=True):
        def fn(e, ps=ps, pairs=pairs, start=start, stop=stop):
            n = len(pairs)
            ins = None
            for i, (l, r) in enumerate(pairs):
                ins = e.matmul(ps, lhsT=l, rhs=r, start=(start and i == 0), stop=(stop and i == n - 1))
            return ins
        SCH.add("pe", fn, reads, writes)

    def transp(ps, src, reads, writes):
        SCH.add("pe", lambda e, ps=ps, src=src: e.transpose(ps, src, ident), reads, writes)

    def act(out, in_, func, reads, writes, bias=0.0, scale=1.0, accum_out=None):
        def fn(e, out=out, in_=in_, func=func, bias=bias, scale=scale, accum_out=accum_out):
            if accum_out is not None:
                return e.activation(out, in_, func, bias=bias, scale=scale, accum_out=accum_out)
            return e.activation(out, in_, func, bias=bias, scale=scale)
        SCH.add("act", fn, reads, writes)

    def ts(eng, out, in0, s1, s2, op0, op1, reads, writes):
        def fn(e, out=out, in0=in0, s1=s1, s2=s2, op0=op0, op1=op1):
            if op1 is None:
                return e.tensor_scalar(out, in0, s1, None, op0)
            return e.tensor_scalar(out, in0, s1, s2, op0, op1)
        SCH.add(eng, fn, reads, writes)

    def stt(eng, out, in0, scalar, in1, op0, op1, reads, writes):
        SCH.add(eng, lambda e, a=(out, in0, scalar, in1, op0, op1): e.scalar_tensor_tensor(*a), reads, writes)

    def tt(eng, out, in0, in1, op, reads, writes):
        SCH.add(eng, lambda e, a=(out, in0, in1, op): e.tensor_tensor(*a), reads, writes)

    def cp(eng, out, in_, reads, writes):
        if eng == "act":
            SCH.add(eng, lambda e, a=(out, in_): e.copy(*a), reads, writes)
        else:
            SCH.add(eng, lambda e, a=(out, in_): e.tensor_copy(*a), reads, writes)

    def memset(eng, ap, val, reads, writes):
        SCH.add(eng, lambda e, a=(ap, val): e.memset(*a), reads, writes)

    def recip(out, in_, reads, writes):
        SCH.add("dve", lambda e, a=(out, in_): e.reciprocal(*a), reads, writes)

    lam_init = [0.8 - 0.6 * math.exp(-0.3 * l) for l in range(2)]

    AR.reset()
    dma("sp", [(CST[:, :], cst_d[:, :])], [], ["CST"], "cst")
    cp("dve", CBF[:, :], CST[:, :], ["CST"], ["CBF"])
    PIN = AR.alloc([7, 128], F32)
    dma("sp", [(PIN[:, b, :], prm_d[b * 128:(b + 1) * 128, :]) for b in range(7)], [], ["PIN"], "pin")
    for b in range(7):
        bank = PS[b % 2]
        transp(bank[:, 0:128], PIN[:, b, :], ["PIN", "CST"], [("ps", b % 2)])
        cp("dve", PT[:, b * 128:(b + 1) * 128], bank[:, 0:128], [("ps", b % 2)], ["PT"])
    dma("sp", [(SG[:, :], sub_d[0, :].partition_broadcast(128))], [], ["SG"], "sg")
    ts("dve", GS[:, :], PT[:, 0:144], float(math.sqrt(D)), None, ALU.mult, None, ["PT"], ["GS"])
    for l in range(2):
        ts("dve", SG[:, l * 256:(l + 1) * 256], SG[:, l * 256:(l + 1) * 256], float(16.0 * (1.0 - lam_init[l])), None,
           ALU.mult, None, ["SG"], ["SG"])
    for l in range(2):
        ts("dve", QKG[:, l:l + 1], PT[:, P_QG + l:P_QG + l + 1], float(math.sqrt(128.0) * SCALE), None, ALU.mult, None,
           ["PT"], ["QKG"])
        ts("dve", QKG[:, 2 + l:3 + l], PT[:, P_KG + l:P_KG + l + 1], float(math.sqrt(128.0)), None, ALU.mult, None,
           ["PT"], ["QKG"])
    for l in range(2):
        for i in range(2):
            a = P_LAM + 4 * l + 2 * i
            tt("dve", LAM[:, 2 * l + i:2 * l + i + 1], PT[:, a:a + 1], PT[:, a + 1:a + 2], ALU.mult, ["PT"], ["LAM"])
    mm(PS[2][:, 0:4], [(ones_f, LAM[:, 0:4])], ["LAM", "CST"], [("ps", 2)])
    act(LAM[:, 4:8], PS[2][:, 0:4], AF.Exp, [("ps", 2)], ["LAM"])
    for l in range(2):
        tt("dve", LAM[:, 8 + l:9 + l], LAM[:, 4 + 2 * l:5 + 2 * l], LAM[:, 5 + 2 * l:6 + 2 * l], ALU.subtract,
           ["LAM"], ["LAM"])
        ts("dve", LAM[:, 10 + l:11 + l], LAM[:, 8 + l:9 + l], float(lam_init[l]), -1.0, ALU.add, ALU.mult,
           ["LAM"], ["LAM"])
    PI = math.pi
    for s in range(NSEQ):
        POSI = AR.alloc([S], I32)
        ANG = AR.alloc([S], F32)
        U = AR.alloc([2, S], F32)
        dma("sp", [(POSI[0:32, :], pos_d[s, :].partition_broadcast(32))], [], [("POSI", s)], "posi")
        cp("dve", ANG[0:32, :], POSI[0:32, :], [("POSI", s)], [("ANG", s)])
        ts("dve", ANG[0:32, :], ANG[0:32, :], CST[0:32, C_INVF:C_INVF + 1], None, ALU.mult, None,
           [("ANG", s), "CST"], [("ANG", s)])
        KI = AR.alloc([S], I32)
        KF = AR.alloc([S], F32)
        XS = AR.alloc([S], F32)
        MM = AR.alloc([S], F32)
        C1 = 6.28125
        C2 = 2 * PI - C1
        for i, sh in enumerate((0.5 * PI, 0.0)):
            ts("dve", XS[0:32, :], ANG[0:32, :], float(sh), None, ALU.add, None, [("ANG", s)], [("XS", s)])
            ts("dve", KI[0:32, :], XS[0:32, :], float(1.0 / (2 * PI)), None, ALU.mult, None, [("XS", s)], [("KI", s)])
            cp("dve", KF[0:32, :], KI[0:32, :], [("KI", s)], [("KF", s)])
            stt("dve", U[0:32, i, :], KF[0:32, :], float(-C1), XS[0:32, :], ALU.mult, ALU.add,
                [("KF", s), ("XS", s)], [("U", s, i)])
            stt("dve", U[0:32, i, :], KF[0:32, :], float(-C2), U[0:32, i, :], ALU.mult, ALU.add,
                [("KF", s), ("U", s, i)], [("U", s, i)])
            ts("dve", MM[0:32, :], U[0:32, i, :], float(PI), float(2 * PI), ALU.is_gt, ALU.mult, [("U", s, i)], [("MM", s)])
            tt("dve", U[0:32, i, :], U[0:32, i, :], MM[0:32, :], ALU.subtract, [("U", s, i), ("MM", s)], [("U", s, i)])
            ts("dve", U[0:32, i, :], U[0:32, i, :], 3.14159, -3.14159, ALU.min, ALU.max, [("U", s, i)], [("U", s, i)])
            act(U[0:32, i, :], U[0:32, i, :], AF.Sin, [("U", s, i)], [("U", s, i)])
        dma("sp", [(cs_d[s, i, :, :], U[0:32, i, :]) for i in range(2)], [("U", s, 0), ("U", s, 1)], [("cs", s)], "csst")
    SCH.barrier()

    wctr = [0]

    def run_jobs(jobs, W):
        NW = len(W)
        wj = [j for j in jobs]
        slots = []
        n = len(wj)
        PREF = NW - 1
        for i in range(n + PREF):
            j = i - PREF
            if j >= 0:
                wj[j][2](slots[j])
            if i < n:
                src, nk, _ = wj[i]
                sl = wctr[0] % NW
                wctr[0] += 1
                pairs = []
                for k0 in range(0, nk, 4):
                    k1 = min(nk, k0 + 4)
                    pairs.append((W[sl][:, k0:k1, :],
                                  src[k0 * 128:k1 * 128, :].rearrange("(k p) n -> p k n", p=128)))
                dma("pool", pairs, [], [("W", sl)], "W%d" % sl)
                slots.append(sl)

    def load_ht(HT, s, t0, q="sp"):
        pairs = [(HT[:, c0:c0 + 4, :], hT_d[s, c0:c0 + 4, :, t0:t0 + TT].rearrange("c p t -> p c t"))
                 for c0 in range(0, NCH, 4)]
        dma(q, pairs, [("hT", s, t0)], ["HT"], "HT")

    def store_ht(HT, s, t0, q="sp"):
        pairs = [(hT_d[s, c0:c0 + 4, :, t0:t0 + TT].rearrange("c p t -> p c t"), HT[:, c0:c0 + 4, :])
                 for c0 in range(0, NCH, 4)]
        dma(q, pairs, ["HT"], [("hT", s, t0)], "HTst")

    def rms_stats(HT, SQ, RS, psb):
        for c in range(NCH):
            sq = SQ[:, c % 2, :]
            act(sq, HT[:, c, :], AF.Square, ["HT"], [("SQ", c % 2)])
            mm(PS[psb][:, :], [(ones_b, sq)], [("SQ", c % 2), "CBF"], [("ps", psb)], start=(c == 0), stop=(c == NCH - 1))
        act(RS[:, :], PS[psb][:, :], AF.Ln, [("ps", psb)], ["RS"], bias=float(EPS * D))
        act(RS[:, :], RS[:, :], AF.Exp, ["RS"], ["RS"], scale=-0.5)

    def rms_apply(XN, xname, HT, RS, gbase):
        for c in range(NCH):
            stt("dve", XN[:, c, :], HT[:, c, :], GS[:, gbase + c:gbase + c + 1], RS[:, :], ALU.mult, ALU.mult,
                ["HT", "RS", "GS"], [xname])

    def phase_proj(l):
        AR.reset()
        HT = AR.alloc([NCH, TT], F32)
        XN = AR.alloc([NCH, TT], BF16)
        XN2 = AR.alloc([NCH, TT], BF16) if l == 2 else None
        W = [AR.alloc([NCH, 512], BF16) for _ in range(4)]
        SQ = AR.alloc([2, TT], BF16)
        RS = AR.alloc([TT], F32)
        RS2 = AR.alloc([2, TT], F32)
        QN = AR.alloc([3, TT], BF16)
        T1 = AR.alloc([2, TT], F32)
        T2 = AR.alloc([2, TT], F32)
        CS = AR.alloc([2, TT], F32)
        VST = AR.alloc([2, NQ, 512], BF16)
        XIN = AR.alloc([2, D], F32) if l == 0 else None
        ctr = {"qn": 0, "vst": 0, "pp": 0}
        jobs = []
        for s in range(NSEQ):
            for ti in range(NTT):
                t0 = ti * TT

                def prologue(s=s, t0=t0):
                    if l == 0:
                        for tq in range(NQ):
                            xi = tq % 2
                            dma("sp", [(XIN[:, xi, :], x_d[s, t0 + tq * 128:t0 + (tq + 1) * 128, :])], [],
                                [("XIN", xi)], "XIN%d" % xi)
                            for b in range(4):
                                pb = b % 2
                                for j in range(4):
                                    c = b * 4 + j
                                    transp(PS[pb][:, j * 128:(j + 1) * 128], XIN[:, xi, c * 128:(c + 1) * 128],
                                           [("XIN", xi), "CST"], [("ps", pb)])
                                cp("act" if b % 2 else "dve", HT[:, b * 4:(b + 1) * 4, tq * 128:(tq + 1) * 128],
                                   PS[pb][:, :].rearrange("p (a b) -> p a b", a=4), [("ps", pb)], ["HT"])
                        store_ht(HT, s, t0)
                    else:
                        load_ht(HT, s, t0)
                    rms_stats(HT, SQ, RS, 2)
                    rms_apply(XN, "XN", HT, RS, P_ATTN + l * 16)
                    if l == 2:
                        rms_apply(XN2, "XN2", HT, RS, P_KV)
                    if l < 2:
                        dma("sp", [(CS[0:32, i, :], cs_d[s, i, :, t0:t0 + TT]) for i in range(2)], [("cs", s)], ["CS"], "CS")

                def qk_group(sl, cg, wsel, s=s, t0=t0, first=False, prologue=prologue):
                    if first:
                        prologue()
                    Wv = W[sl]
                    xn = XN2 if wsel == "bk" else XN
                    xname = "XN2" if wsel == "bk" else "XN"
                    for j in range(4):
                        hc = cg * 4 + j
                        pb = ctr["pp"] % 2
                        ctr["pp"] += 1
                        mm(PS[pb][:, :], [(Wv[:, k, j * 128:(j + 1) * 128], xn[:, k, :]) for k in range(NCH)],
                           [("W", sl), xname], [("ps", pb)])
                        qi = ctr["qn"] % 3
                        ctr["qn"] += 1
                        qn = QN[:, qi, :]
                        if wsel in ("aq", "ak"):
                            sq = SQ[:, pb, :]
                            act(sq, PS[pb][:, :], AF.Square, [("ps", pb)], [("SQ", pb)])
                            mm(PS[4 + pb][:, :], [(ones_b, sq)], [("SQ", pb), "CBF"], [("ps", 4 + pb)])
                            act(RS2[:, pb, :], PS[4 + pb][:, :], AF.Ln, [("ps", 4 + pb)], [("RS2", pb)], bias=float(EPS * 128))
                            act(RS2[:, pb, :], RS2[:, pb, :], AF.Exp, [("RS2", pb)], [("RS2", pb)], scale=-0.5)
                            gcol = (0 if wsel == "aq" else 2) + l
                            stt("dve", qn, PS[pb][:, :], QKG[:, gcol:gcol + 1], RS2[:, pb, :], ALU.mult, ALU.mult,
                                [("ps", pb), ("RS2", pb), "QKG"], [("QN", qi)])
                            mm(PS[6 + pb][0:32, :], [(rrot_b, qn[0:32, :])], [("QN", qi), "CBF"], [("ps", 6 + pb)])
                            tt("dve", T1[0:32, pb, :], PS[6 + pb][0:32, :], CS[0:32, 1, :], ALU.mult,
                               [("ps", 6 + pb), "CS"], [("T1", pb)])
                            tt("pool", T2[0:32, pb, :], qn[0:32, :], CS[0:32, 0, :], ALU.mult,
                               [("QN", qi), "CS"], [("T2", pb)])
                            tt("pool", qn[0:32, :], T1[0:32, pb, :], T2[0:32, pb, :], ALU.add,
                               [("T1", pb), ("T2", pb)], [("QN", qi)])
                        elif wsel == "bq":
                            SCH.add("act", lambda e, a=(qn, PS[pb][:, :], float(SCALE)): e.mul(*a), [("ps", pb)], [("QN", qi)])
                        else:
                            cp("act", qn, PS[pb][:, :], [("ps", pb)], [("QN", qi)])
                        dst = kT_d if wsel in ("ak", "bk") else qT_d
                        dma("sp", [(dst[s, hc, :, t0:t0 + TT], qn)], [("QN", qi)], [("qk", wsel, s, hc, t0)], "QN%d" % qi)

                def v_group(sl, cg, xsel, s=s, t0=t0):
                    Wv = W[sl]
                    xn = XN2 if xsel == "XN2" else XN
                    vi = ctr["vst"] % 2
                    ctr["vst"] += 1
                    for tq in range(NQ):
                        pb = ctr["pp"] % 2
                        ctr["pp"] += 1
                        mm(PS[pb][:, :], [(xn[:, k, tq * 128:(tq + 1) * 128], Wv[:, k, :]) for k in range(NCH)],
                           [("W", sl), xsel], [("ps", pb)])
                        cp("act", VST[:, vi, tq, :], PS[pb][:, :], [("ps", pb)], [("VST", vi)])
                    dma("sp", [(v_d[s, t0:t0 + TT, cg * 512:(cg + 1) * 512].rearrange("(q p) e -> p q e", p=128),
                                VST[:, vi, :, :])], [("VST", vi)], [("v", s, t0, cg)], "VST%d" % vi)

                if l < 2:
                    for cg in range(8):
                        wsel = "aq" if cg < 4 else "ak"
                        jobs.append((wqkv_d[l, :, cg * 512:(cg + 1) * 512], NCH,
                                     lambda sl, cg=cg, wsel=wsel, f=qk_group, first=(cg == 0): f(sl, cg % 4, wsel, first=first)))
                    for cg in range(4):
                        jobs.append((wqkv_d[l, :, (8 + cg) * 512:(9 + cg) * 512], NCH,
                                     lambda sl, cg=cg, f=v_group: f(sl, cg, "XN")))
                else:
                    jb = l - 2
                    for cg in range(4):
                        jobs.append((bwq_d[jb, :, cg * 512:(cg + 1) * 512], NCH,
                                     lambda sl, cg=cg, f=qk_group, first=(cg == 0): f(sl, cg, "bq", first=first)))
                    if l == 2:
                        for cg in range(4):
                            jobs.append((bkv_d[:, cg * 512:(cg + 1) * 512], NCH,
                                         lambda sl, cg=cg, f=qk_group: f(sl, cg, "bk")))
                        for cg in range(4):
                            jobs.append((bkv_d[:, (4 + cg) * 512:(5 + cg) * 512], NCH,
                                         lambda sl, cg=cg, f=v_group: f(sl, cg, "XN2")))
        run_jobs(jobs, W)
        SCH.barrier()

    def tri_off(kt):
        return kt * S - 128 * (kt * (kt - 1) // 2)

    TRI = tri_off(NT)

    def chunks(lo, hi):
        c = []
        while lo < hi:
            n = min(512, hi - lo)
            c.append((lo, n))
            lo += n
        return c

    def phase_attn_a(l):
        AR.reset()
        QK = [AR.alloc([4, S], BF16) for _ in range(2)]
        VE = [AR.alloc([NT, 258], BF16) for _ in range(2)]
        PTS = [AR.alloc([TRI], BF16) for _ in range(2)]
        RR = AR.alloc([2, 4], F32)
        T1 = AR.alloc([2, 256], F32)
        OO = AR.alloc([2, 256], F32)
        OF = AR.alloc([2, 256], F32)
        JK = AR.alloc([256], F32)
        OTS = AR.alloc([2, 2, TT], BF16)
        for i in range(2):
            memset("pool", VE[i][:, :, 256:258], 1.0, [], [("VE", i)])
        hi = 0
        pp = 0
        fin = 0
        for s in range(NSEQ):
            for h in range(8):
                b = hi % 2
                hi += 1
                dma("sp", [(QK[b][:, 0, :], qT_d[s, 2 * h, :, :]), (QK[b][:, 1, :], qT_d[s, 2 * h + 1, :, :]),
                           (QK[b][:, 2, :], kT_d[s, 2 * h, :, :]), (QK[b][:, 3, :], kT_d[s, 2 * h + 1, :, :])],
                    [], [("QK", b)], "QK%d" % b)
                vp = []
                for k0 in range(0, NT, 4):
                    k1 = min(NT, k0 + 4)
                    vp.append((VE[b][:, k0:k1, 0:256],
                               v_d[s, k0 * 128:k1 * 128, h * 256:(h + 1) * 256].rearrange("(t p) e -> p t e", p=128)))
                dma("sp", vp, [], [("VE", b)], "VE%d" % b)
                for c in range(2):
                    for kt in range(NT):
                        for (q0, n) in chunks(kt * 128, S):
                            pb = pp % 2
                            pp += 1
                            mm(PS[pb][:, 0:n], [(QK[b][:, 2 + c, kt * 128:(kt + 1) * 128], QK[b][:, c, q0:q0 + n])],
                               [("QK", b)], [("ps", pb)])
                            o = tri_off(kt) + (q0 - kt * 128)
                            act(PTS[c][:, o:o + n], PS[pb][:, 0:n], AF.Exp, [("ps", pb)], [("PTS", c, kt)])
                        o = tri_off(kt)
                        memset("pool", PTS[c][64:128, o:o + 64], 0.0, [], [("PTS", c, kt)])
                for qt in range(NT):
                    f = fin % 2
                    fin += 1
                    for c in range(2):
                        pb = 2 + 2 * f + c
                        mm(PS[pb][:, 0:257],
                           [(PTS[c][:, tri_off(kt) + (qt - kt) * 128: tri_off(kt) + (qt - kt + 1) * 128],
                             VE[b][:, kt, 0:257]) for kt in range(qt + 1)],
                           [("PTS", c, kt) for kt in range(qt + 1)] + [("VE", b)], [("ps", pb)])
                    p0, p1 = PS[2 + 2 * f], PS[3 + 2 * f]
                    recip(RR[:, f, 0:1], p0[:, 256:257], [("ps", 2 + 2 * f)], [("RR", f)])
                    recip(RR[:, f, 1:2], p1[:, 256:257], [("ps", 3 + 2 * f)], [("RR", f)])
                    tt("dve", RR[:, f, 2:3], RR[:, f, 1:2], LAM[:, 10 + l:11 + l], ALU.mult, [("RR", f), "LAM"], [("RR", f)])
                    ts("dve", T1[:, f, :], p1[:, 0:256], RR[:, f, 2:3], None, ALU.mult, None,
                       [("ps", 3 + 2 * f), ("RR", f)], [("T1", f)])
                    stt("dve", OO[:, f, :], p0[:, 0:256], RR[:, f, 0:1], T1[:, f, :], ALU.mult, ALU.add,
                        [("ps", 2 + 2 * f), ("RR", f), ("T1", f)], [("OO", f)])
                    memset("pool", RR[:, f, 3:4], 0.0, [], [("RS3", f)])
                    act(JK[:, :], OO[:, f, :], AF.Square, [("OO", f), ("RS3", f)], ["JK", ("RS3", f)],
                        accum_out=RR[:, f, 3:4])
                    act(RR[:, f, 3:4], RR[:, f, 3:4], AF.Ln, [("RS3", f)], [("RS3", f)], bias=float(EPS * 256))
                    act(RR[:, f, 3:4], RR[:, f, 3:4], AF.Exp, [("RS3", f)], [("RS3", f)], scale=-0.5)
                    stt("dve", OF[:, f, :], OO[:, f, :], RR[:, f, 3:4], SG[:, l * 256:(l + 1) * 256], ALU.mult, ALU.mult,
                        [("OO", f), ("RS3", f), "SG"], [("OF", f)])
                    tb = 6 + f
                    for i in range(2):
                        transp(PS[tb][:, i * 128:(i + 1) * 128], OF[:, f, i * 128:(i + 1) * 128], [("OF", f), "CST"],
                               [("ps", tb)])
                    g = qt // NQ
                    ob = g % 2
                    cp("act", OTS[:, ob, :, (qt % NQ) * 128:(qt % NQ + 1) * 128],
                       PS[tb][:, 0:256].rearrange("p (a b) -> p a b", a=2), [("ps", tb)], [("OTS", ob)])
                    if qt % NQ == NQ - 1:
                        dma("sp", [(oT_d[s, 2 * h:2 * h + 2, :, g * TT:(g + 1) * TT].rearrange("c p t -> p c t"),
                                    OTS[:, ob, :, :])], [("OTS", ob)], [("oT", s, h, g)], "OTS%d" % ob)
        SCH.barrier()

    def phase_attn_b():
        AR.reset()
        QB = [AR.alloc([2, S], BF16) for _ in range(2)]
        VB = [AR.alloc([NT, 128], BF16) for _ in range(2)]
        SP = AR.alloc([TRI], BF16)
        SS = AR.alloc([TRI], BF16)
        ACC = AR.alloc([S], F32)
        EE = AR.alloc([2, 512], F32)
        AT = AR.alloc([3, 512], BF16)
        OTS = AR.alloc([2, TT], BF16)
        hi = 0
        pp = 0
        lp = 0
        ai = 0
        oi = 0
        for s in range(NSEQ):
            for h in range(16):
                b = hi % 2
                hi += 1
                dma("sp", [(QB[b][:, 0, :], qT_d[s, h, :, :]), (QB[b][:, 1, :], kT_d[s, h, :, :])],
                    [], [("QB", b)], "QB%d" % b)
                vp = []
                for k0 in range(0, NT, 4):
                    k1 = min(NT, k0 + 4)
                    vp.append((VB[b][:, k0:k1, :],
                               v_d[s, k0 * 128:k1 * 128, h * 128:(h + 1) * 128].rearrange("(t p) e -> p t e", p=128)))
                dma("sp", vp, [], [("VB", b)], "VB%d" % b)
                for kt in range(NT):
                    for (q0, n) in chunks(kt * 128, S):
                        pb = pp % 2
                        pp += 1
                        mm(PS[pb][:, 0:n], [(QB[b][:, 1, kt * 128:(kt + 1) * 128], QB[b][:, 0, q0:q0 + n])],
                           [("QB", b)], [("ps", pb)])
                        act(EE[:, pb, 0:n], PS[pb][:, 0:n], AF.Exp, [("ps", pb)], [("EE", pb)])
                        o = tri_off(kt) + (q0 - kt * 128)
                        act(SP[:, o:o + n], EE[:, pb, 0:n], AF.Ln, [("EE", pb)], [("SP", kt)], bias=1.0)
                    o = tri_off(kt)
                    tt("pool", SP[:, o:o + 128], SP[:, o:o + 128], maskb_b, ALU.mult, [("SP", kt), "CBF"], [("SP", kt)])
                memset("pool", ACC[:, :], 0.0, [], ["ACC"])
                for kt in range(NT - 2, -1, -1):
                    lo = (kt + 1) * 128
                    o1 = tri_off(kt + 1)
                    tt("pool", ACC[:, lo:S], ACC[:, lo:S], SP[:, o1:o1 + (S - lo)], ALU.add, ["ACC", ("SP", kt + 1)], ["ACC"])
                    o = tri_off(kt)
                    cp("pool", SS[:, o + 128:o + 128 + (S - lo)], ACC[:, lo:S], ["ACC"], [("SS", kt)])
                for Q in range(S // 512):
                    ob = 4 + (oi % 2)
                    oi += 1
                    ktmax = min(NT - 1, 4 * Q + 3)
                    for kt in range(ktmax + 1):
                        q0 = max(Q * 512, kt * 128)
                        n = (Q + 1) * 512 - q0
                        lo = q0 - Q * 512
                        lb = 2 + (lp % 2)
                        lp += 1
                        o = tri_off(kt) + (q0 - kt * 128)
                        pairs = [(QB[b][:, 1, kt * 128:(kt + 1) * 128], QB[b][:, 0, q0:q0 + n]),
                                 (ntri_b, SP[:, o:o + n])]
                        rds = [("QB", b), ("SP", kt), "CBF"]
                        q1 = max(q0, (kt + 1) * 128)
                        has_ss = q1 < (Q + 1) * 512 and kt < NT - 1
                        mm(PS[lb][:, lo:512], pairs, rds, [("ps", lb)], start=True, stop=not has_ss)
                        if has_ss:
                            o2 = tri_off(kt) + (q1 - kt * 128)
                            mm(PS[lb][:, q1 - Q * 512:512], [(nones_b, SS[:, o2:o2 + (Q + 1) * 512 - q1])],
                               [("SS", kt), "CBF"], [("ps", lb)], start=False, stop=True)
                        a = ai % 3
                        ai += 1
                        act(AT[:, a, 0:n], PS[lb][:, lo:512], AF.Exp, [("ps", lb)], [("AT", a)])
                        if kt * 128 >= Q * 512:
                            tt("pool", AT[:, a, 0:128], AT[:, a, 0:128], maskb_b, ALU.mult, [("AT", a), "CBF"], [("AT", a)])
                        mm(PS[ob][:, lo:512], [(VB[b][:, kt, :], AT[:, a, 0:n])], [("VB", b), ("AT", a)], [("ps", ob)],
                           start=(kt == 0), stop=(kt == ktmax))
                    sb_ = oi % 2
                    cp("dve", OTS[:, sb_, :], PS[ob][:, :], [("ps", ob)], [("OTS", sb_)])
                    dma("sp", [(oT_d[s, h, :, Q * 512:(Q + 1) * 512], OTS[:, sb_, :])], [("OTS", sb_)], [("oT", s, h, Q)],
                        "OTS%d" % sb_)
        SCH.barrier()

    def phase_ffn(l, last):
        AR.reset()
        HT = AR.alloc([NCH, TT], F32)
        XN = AR.alloc([NCH, TT], BF16)
        W = [AR.alloc([NCH, 512], BF16) for _ in range(3)]
        ACTB = AR.alloc([NFF, TT], BF16)
        SQ = AR.alloc([2, TT], BF16)
        RS = AR.alloc([TT], F32)
        GE = AR.alloc([2, TT + 2], F32)
        A1 = AR.alloc([2, TT], F32)
        A2 = AR.alloc([2, TT], F32)
        XO = AR.alloc([D], F32) if last else None
        wo_src = awo_d[l] if l < 2 else bwo_d[l - 2]
        cwb = P_CW + l * 3 * NFF
        cbb = P_CB + l * NFF
        ctr = {"pp": 0, "ge": 0}
        jobs = []
        for s in range(NSEQ):
            for ti in range(NTT):
                t0 = ti * TT

                def prologue(s=s, t0=t0):
                    load_ht(HT, s, t0)
                    dma("sp", [(XN[:, c0:c0 + 4, :], oT_d[s, c0:c0 + 4, :, t0:t0 + TT].rearrange("c p t -> p c t"))
                               for c0 in range(0, NCH, 4)], [], ["XN"], "XN")

                def oproj(sl, cg, first, s=s, t0=t0, ti=ti, prologue=prologue):
                    if first:
                        prologue()
                    for j in range(4):
                        oc = cg * 4 + j
                        pb = ctr["pp"] % 2
                        ctr["pp"] += 1
                        mm(PS[pb][:, :], [(W[sl][:, k, j * 128:(j + 1) * 128], XN[:, k, :]) for k in range(NCH)],
                           [("W", sl), "XN"], [("ps", pb)])
                        tt("dve", HT[:, oc, :], HT[:, oc, :], PS[pb][:, :], ALU.add, ["HT", ("ps", pb)], ["HT"])
                    if cg == 3:
                        rms_stats(HT, SQ, RS, 2)
                        rms_apply(XN, "XN", HT, RS, P_FFN + l * 16)

                def up(slu, slg, gi, s=s, t0=t0, ti=ti):
                    for j in range(4):
                        c = gi * 4 + j
                        pb = ctr["pp"] % 2
                        ctr["pp"] += 1
                        pu, pg = PS[pb], PS[2 + pb]
                        mm(pu[:, :], [(W[slu][:, k, j * 128:(j + 1) * 128], XN[:, k, :]) for k in range(NCH)],
                           [("W", slu), "XN"], [("ps", pb)])
                        mm(pg[:, :], [(W[slg][:, k, j * 128:(j + 1) * 128], XN[:, k, :]) for k in range(NCH)],
                           [("W", slg), "XN"], [("ps", 2 + pb)])
                        gi_ = ctr["ge"] % 2
                        ctr["ge"] += 1
                        ge = GE[:, gi_, :]
                        if ti == 0:
                            memset("pool", ge[:, 0:2], 0.0, [], [("GE", gi_)])
                        else:
                            cp("pool", ge[:, 0:2], HALO[:, c, :], [("HALO", c)], [("GE", gi_)])
                        cp("act", ge[:, 2:TT + 2], pg[:, :], [("ps", 2 + pb)], [("GE", gi_)])
                        cp("pool", HALO[:, c, :], ge[:, TT:TT + 2], [("GE", gi_)], [("HALO", c)])
                        w0 = PT[:, cwb + 0 * NFF + c:cwb + 0 * NFF + c + 1]
                        w1 = PT[:, cwb + 1 * NFF + c:cwb + 1 * NFF + c + 1]
                        w2 = PT[:, cwb + 2 * NFF + c:cwb + 2 * NFF + c + 1]
                        bb = PT[:, cbb + c:cbb + c + 1]
                        ts("dve", A1[:, gi_, :], ge[:, 0:TT], w0, bb, ALU.mult, ALU.add, [("GE", gi_), "PT"], [("A1", gi_)])
                        stt("dve", A2[:, gi_, :], ge[:, 1:TT + 1], w1, A1[:, gi_, :], ALU.mult, ALU.add,
                            [("GE", gi_), ("A1", gi_), "PT"], [("A2", gi_)])
                        stt("dve", A1[:, gi_, :], ge[:, 2:TT + 2], w2, A2[:, gi_, :], ALU.mult, ALU.add,
                            [("GE", gi_), ("A2", gi_), "PT"], [("A1", gi_)])
                        act(A2[:, gi_, :], A1[:, gi_, :], AF.Silu, [("A1", gi_)], [("A2", gi_)])
                        tt("dve", ACTB[:, c, :], A2[:, gi_, :], pu[:, :], ALU.mult, [("A2", gi_), ("ps", pb)], [("ACTB", c)])

                def down(sl, cg, kp, s=s, t0=t0, ti=ti):
                    for j in range(4):
                        pd = 4 + j
                        mm(PS[pd][:, :], [(W[sl][:, kk, j * 128:(j + 1) * 128], ACTB[:, kp * 11 + kk, :]) for kk in range(11)],
                           [("W", sl)] + [("ACTB", kp * 11 + kk) for kk in range(11)], [("ps", pd)],
                           start=(kp == 0), stop=(kp == 3))
                    if kp == 3:
                        for j in range(4):
                            oc = cg * 4 + j
                            tt("dve", HT[:, oc, :], HT[:, oc, :], PS[4 + j][:, :], ALU.add, ["HT", ("ps", 4 + j)], ["HT"])
                        if cg == 3:
                            if not last:
                                store_ht(HT, s, t0)
                            else:
                                for tq in range(NQ):
                                    for b4 in range(4):
                                        pb = b4 % 2
                                        for j in range(4):
                                            c = b4 * 4 + j
                                            transp(PS[pb][:, j * 128:(j + 1) * 128], HT[:, c, tq * 128:(tq + 1) * 128],
                                                   ["HT", "CST"], [("ps", pb)])
                                        cp("act" if b4 % 2 else "dve", XO[:, b4 * 512:(b4 + 1) * 512], PS[pb][:, :],
                                           [("ps", pb)], ["XO"])
                                    dma("sp", [(out_d[s, t0 + tq * 128:t0 + (tq + 1) * 128, :], XO[:, :])], ["XO"],
                                        [("out", s, t0, tq)], "XO")

                for cg in range(4):
                    jobs.append((wo_src[:, cg * 512:(cg + 1) * 512], NCH,
                                 lambda sl, cg=cg, f=oproj: f(sl, cg, cg == 0)))
                for gi in range(11):
                    holder = {}
                    jobs.append((wup_d[l, :, gi * 512:(gi + 1) * 512], NCH,
                                 lambda sl, holder=holder: holder.__setitem__("u", sl)))
                    jobs.append((wup_d[l, :, DFF + gi * 512:DFF + (gi + 1) * 512], NCH,
                                 lambda sl, gi=gi, holder=holder, f=up: f(holder["u"], sl, gi)))
                for cg in range(4):
                    for kp in range(4):
                        jobs.append((wdn_d[l, kp * 1408:(kp + 1) * 1408, cg * 512:(cg + 1) * 512], 11,
                                     lambda sl, cg=cg, kp=kp, f=down: f(sl, cg, kp)))
        run_jobs(jobs, W)
        SCH.barrier()

    for l in range(n_layers):
        phase_proj(l)
        if l < 2:
            phase_attn_a(l)
        else:
            phase_attn_b()
        phase_ffn(l, last=(l == n_layers - 1))

    with ExitStack() as stack:
        SCH.emit(nc, stack)
    return nc, SCH


W_NAMES = ["a_w_qkv", "a_w_o", "b_w_kv", "b_w_q", "b_w_o", "ffn_w_up", "ffn_w_down"]


def make_in_maps(inputs, n_cores, nseq, S):
    f = lambda a: np.ascontiguousarray(np.asarray(a), dtype=np.float32)
    cst = make_consts()
    prm = pack_params(f(inputs["attn_norm_g"]), f(inputs["ffn_norm_g"]), f(inputs["kv_norm_g"]),
                      f(inputs["ffn_conv_w"]), f(inputs["ffn_conv_b"]), f(inputs["a_q_norm_g"]),
                      f(inputs["a_k_norm_g"]), f(inputs["a_lambda_q1"]), f(inputs["a_lambda_k1"]),
                      f(inputs["a_lambda_q2"]), f(inputs["a_lambda_k2"]))
    sub = f(inputs["a_subln_g"]).reshape(1, 512)
    ws = {k: f(inputs[k]) for k in W_NAMES}
    x = np.asarray(inputs["x"])
    pos = np.asarray(inputs["positions"]).astype(np.int32)
    maps = []
    for c in range(n_cores):
        m = {"x": np.ascontiguousarray(x[c * nseq:(c + 1) * nseq, :S], dtype=np.float32),
             "pos": np.ascontiguousarray(pos[c * nseq:(c + 1) * nseq, :S]),
             "cst": cst, "prm": prm, "subln": sub}
        m.update(ws)
        maps.append(m)
    return maps


def kernel(**inputs):
    n_cores, nseq, S = 8, 2, 2048
    nc, _ = build(S=S, NSEQ=nseq)
    maps = make_in_maps(inputs, n_cores, nseq, S)
    res = run_bass_kernel_spmd(nc, maps, core_ids=list(range(n_cores)))
    return np.concatenate([np.asarray(r["out"], dtype=np.float32) for r in res.results], axis=0)
```

```python
import math
from contextlib import ExitStack

import numpy as np
import concourse.bass as bass
import concourse.mybir as mybir
from concourse.bass_utils import run_bass_kernel_spmd

F32 = mybir.dt.float32
BF16 = mybir.dt.bfloat16
I32 = mybir.dt.int32
AF = mybir.ActivationFunctionType
ALU = mybir.AluOpType
AX = mybir.AxisListType

D = 2048
NCH = 16
DFF = 5632
NFF = 44
TT = 512
EPS = 1e-6
SCALE = 128 ** -0.5
SEM_EPOCH = 30000


class Op:
    __slots__ = ("eng", "fn", "deps", "ndma", "key", "sig", "awaited", "idx", "join")


class Sched:
    def __init__(self):
        self.ops = []
        self.last_w = {}
        self.readers = {}
        self.last_op = {}

    def barrier(self):
        deps = sorted(set(self.last_op.values()))
        for e in ("pe", "act", "dve", "pool", "sp"):
            idx = self.add(e, lambda eh: None, join=True)
            op = self.ops[idx]
            op.deps = list(deps)
            for d in deps:
                self.ops[d].awaited = True
        self.last_w = {}
        self.readers = {}

    def add(self, eng, fn, reads=(), writes=(), ndma=0, key=None, join=False):
        idx = len(self.ops)
        deps = set()
        for r in reads:
            w = self.last_w.get(r)
            if w is not None:
                deps.add(w)
        for r in writes:
            w = self.last_w.get(r)
            if w is not None:
                deps.add(w)
            for rd in self.readers.get(r, ()):
                deps.add(rd)
        for r in reads:
            self.readers.setdefault(r, []).append(idx)
        for r in writes:
            self.last_w[r] = idx
            self.readers[r] = []
        op = Op()
        op.eng, op.fn, op.ndma, op.key, op.idx = eng, fn, ndma, key, idx
        op.awaited = False
        op.sig = None
        op.join = join
        if not join:
            self.last_op[("d", key) if ndma else ("e", eng)] = idx
        if ndma == 0 and eng == "pe":
            deps = {d for d in deps if not (self.ops[d].eng == "pe" and self.ops[d].ndma == 0)}
        deps.discard(idx)
        op.deps = sorted(deps)
        for d in op.deps:
            self.ops[d].awaited = True
        self.ops.append(op)
        return idx

    def emit(self, nc, stack):
        engs = ["pe", "act", "dve", "pool", "sp"]
        sems = {}

        def get_sem(name):
            if name not in sems:
                sems[name] = stack.enter_context(nc.semaphore(name))
            return sems[name]

        cnt = {e: [0, 0] for e in engs}
        dcnt = {}
        for op in self.ops:
            if op.ndma:
                st = dcnt.setdefault(op.key, [0, 0])
                inc = 16 * op.ndma
                if st[1] + inc > SEM_EPOCH:
                    st[0] += 1
                    st[1] = 0
                st[1] += inc
                op.sig = ("d_%s_%d" % (op.key, st[0]), st[1])
            elif op.awaited:
                st = cnt[op.eng]
                if st[1] + 1 > SEM_EPOCH:
                    st[0] += 1
                    st[1] = 0
                st[1] += 1
                op.sig = ("e_%s_%d" % (op.eng, st[0]), st[1])
        for op in self.ops:
            if op.sig is not None:
                get_sem(op.sig[0])
        self.nsems = len(sems)
        per_eng = {e: [op for op in self.ops if op.eng == e] for e in engs}
        ops = self.ops

        def run_engine(ename, eh):
            waited = {}
            for op in per_eng[ename]:
                need = {}
                for d in op.deps:
                    s = ops[d].sig
                    if need.get(s[0], 0) < s[1]:
                        need[s[0]] = s[1]
                for sname, v in need.items():
                    if waited.get(sname, 0) >= v:
                        continue
                    eh.wait_ge(sems[sname], v)
                    waited[sname] = v
                r = op.fn(eh)
                if r is None:
                    assert op.sig is None
                    continue
                if op.ndma:
                    sem = sems[op.sig[0]]
                    assert len(r) == op.ndma
                    for ins in r:
                        ins.then_inc(sem, 16)
                elif op.sig is not None:
                    r.then_inc(sems[op.sig[0]], 1)

        with nc.Block() as block:
            @block.tensor
            def _(e):
                run_engine("pe", e)

            @block.scalar
            def _(e):
                run_engine("act", e)

            @block.vector
            def _(e):
                run_engine("dve", e)

            @block.gpsimd
            def _(e):
                run_engine("pool", e)

            @block.sync
            def _(e):
                run_engine("sp", e)


C_ID, C_ONES, C_NTRI, C_NONES, C_MASKB, C_RROT, C_INVF, NCST = 0, 128, 256, 384, 512, 640, 672, 680
P_ATTN, P_FFN, P_KV, P_CW, P_CB, P_QG, P_KG, P_LAM, NPROW = 0, 64, 128, 144, 672, 848, 850, 852, 896


def make_consts():
    c = np.zeros((128, NCST), np.float32)
    c[:, C_ID:C_ID + 128] = np.eye(128, dtype=np.float32)
    c[:, C_ONES:C_ONES + 128] = 1.0
    kk = np.arange(128)
    c[:, C_NTRI:C_NTRI + 128] = -(kk[:, None] >= kk[None, :]).astype(np.float32)
    c[:, C_NONES:C_NONES + 128] = -1.0
    c[:, C_MASKB:C_MASKB + 128] = (kk[None, :] > kk[:, None]).astype(np.float32)
    r = np.zeros((128, 32), np.float32)
    for d in range(16):
        r[d + 16, d] = -1.0
        r[d, d + 16] = 1.0
    c[:, C_RROT:C_RROT + 32] = r
    inv = (np.float32(500000.0) ** (-np.arange(0, 32, 2, dtype=np.float32) / np.float32(32))).astype(np.float32)
    c[0:16, C_INVF] = inv
    c[16:32, C_INVF] = inv
    return c


def pack_params(attn_norm_g, ffn_norm_g, kv_norm_g, ffn_conv_w, ffn_conv_b, a_q_norm_g, a_k_norm_g,
                lq1, lk1, lq2, lk2):
    p = np.zeros((NPROW, 128), np.float32)
    p[P_ATTN:P_ATTN + 64] = attn_norm_g.reshape(64, 128)
    p[P_FFN:P_FFN + 64] = ffn_norm_g.reshape(64, 128)
    p[P_KV:P_KV + 16] = kv_norm_g.reshape(16, 128)
    p[P_CW:P_CW + 528] = ffn_conv_w.reshape(4 * 3 * NFF, 128)
    p[P_CB:P_CB + 176] = ffn_conv_b.reshape(4 * NFF, 128)
    p[P_QG:P_QG + 2] = a_q_norm_g
    p[P_KG:P_KG + 2] = a_k_norm_g
    for l in range(2):
        p[P_LAM + 4 * l + 0] = lq1[l]
        p[P_LAM + 4 * l + 1] = lk1[l]
        p[P_LAM + 4 * l + 2] = lq2[l]
        p[P_LAM + 4 * l + 3] = lk2[l]
    return p


def build(S=2048, NSEQ=2, n_layers=4, dbg=False):
    nc = bass.Bass("TRN2", target_bir_lowering=False)
    NT = S // 128
    NTT = S // TT
    NQ = TT // 128
    SCH = Sched()
    kind_dbg = "ExternalOutput" if dbg else "Internal"

    x_d = nc.dram_tensor("x", [NSEQ, S, D], F32, kind="ExternalInput")
    pos_d = nc.dram_tensor("pos", [NSEQ, S], I32, kind="ExternalInput")
    cst_d = nc.dram_tensor("cst", [128, NCST], F32, kind="ExternalInput")
    prm_d = nc.dram_tensor("prm", [NPROW, 128], F32, kind="ExternalInput")
    sub_d = nc.dram_tensor("subln", [1, 512], F32, kind="ExternalInput")
    wqkv_d = nc.dram_tensor("a_w_qkv", [2, D, 3 * D], F32, kind="ExternalInput")
    awo_d = nc.dram_tensor("a_w_o", [2, D, D], F32, kind="ExternalInput")
    bkv_d = nc.dram_tensor("b_w_kv", [D, 2 * D], F32, kind="ExternalInput")
    bwq_d = nc.dram_tensor("b_w_q", [2, D, D], F32, kind="ExternalInput")
    bwo_d = nc.dram_tensor("b_w_o", [2, D, D], F32, kind="ExternalInput")
    wup_d = nc.dram_tensor("ffn_w_up", [4, D, 2 * DFF], F32, kind="ExternalInput")
    wdn_d = nc.dram_tensor("ffn_w_down", [4, DFF, D], F32, kind="ExternalInput")
    out_d = nc.dram_tensor("out", [NSEQ, S, D], F32, kind="ExternalOutput")

    hT_d = nc.dram_tensor("hT", [NSEQ, NCH, 128, S], F32, kind=kind_dbg)
    cs_d = nc.dram_tensor("cs", [NSEQ, 2, 32, S], F32, kind=kind_dbg)
    qT_d = nc.dram_tensor("qT", [NSEQ, NCH, 128, S], BF16, kind=kind_dbg)
    kT_d = nc.dram_tensor("kT", [NSEQ, NCH, 128, S], BF16, kind=kind_dbg)
    v_d = nc.dram_tensor("v", [NSEQ, S, D], BF16, kind=kind_dbg)
    oT_d = nc.dram_tensor("oT", [NSEQ, NCH, 128, S], BF16, kind=kind_dbg)

    def sb(name, shape, dt):
        return nc.alloc_sbuf_tensor(name, [128] + list(shape), dt)

    CST = sb("CST", [NCST], F32)
    CBF = sb("CBF", [NCST], BF16)
    PT = sb("PT", [NPROW], F32)
    GS = sb("GS", [144], F32)
    SG = sb("SG", [512], F32)
    LAM = sb("LAM", [16], F32)
    QKG = sb("QKG", [4], F32)
    HALO = sb("HALO", [NFF, 2], F32)
    ARENA_WORDS = 44000
    ARENA = sb("ARENA", [ARENA_WORDS], F32)
    PS = [nc.alloc_psum_tensor("ps%d" % i, [128, 512], F32) for i in range(8)]

    ident = CST[:, C_ID:C_ID + 128]
    ones_f = CST[:, C_ONES:C_ONES + 128]
    ones_b = CBF[:, C_ONES:C_ONES + 128]
    ntri_b = CBF[:, C_NTRI:C_NTRI + 128]
    nones_b = CBF[:, C_NONES:C_NONES + 128]
    maskb_b = CBF[:, C_MASKB:C_MASKB + 128]
    rrot_b = CBF[0:32, C_RROT:C_RROT + 32]

    class Arena:
        def __init__(self):
            self.off = 0

        def reset(self):
            self.off = 0

        def alloc(self, shape, dt):
            n = int(np.prod(shape))
            words = (n * (2 if dt == BF16 else 4) + 3) // 4
            words = (words + 7) // 8 * 8
            assert self.off + words <= ARENA_WORDS, ("arena overflow", self.off, words)
            v = ARENA[:, self.off:self.off + words]
            self.off += words
            if dt == BF16:
                v = v.bitcast(BF16)
            elif dt == I32:
                v = v.bitcast(I32)
            v = v[:, 0:n]
            if len(shape) == 2:
                v = v.rearrange("p (a b) -> p a b", a=shape[0])
            elif len(shape) == 3:
                v = v.rearrange("p (a b c) -> p a b c", a=shape[0], b=shape[1])
            return v

    AR = Arena()

    def dma(q, pairs, reads, writes, key):
        def fn(e, pairs=pairs):
            return [e.dma_start(out=d, in_=s) for d, s in pairs]
        SCH.add(q, fn, reads, writes, ndma=len(pairs), key=key)

    def mm(ps, pairs, reads, writes, start=True, stop=True):
        def fn(e, ps=ps, pairs=pairs, start=start, stop=stop):
            n = len(pairs)
            ins = None
            for i, (l, r) in enumerate(pairs):
                ins = e.matmul(ps, lhsT=l, rhs=r, start=(start and i == 0), stop=(stop and i == n - 1))
            return ins
        SCH.add("pe", fn, reads, writes)

    def transp(ps, src, reads, writes):
        SCH.add("pe", lambda e, ps=ps, src=src: e.transpose(ps, src, ident), reads, writes)

    def act(out, in_, func, reads, writes, bias=0.0, scale=1.0, accum_out=None):
        def fn(e, out=out, in_=in_, func=func, bias=bias, scale=scale, accum_out=accum_out):
            if accum_out is not None:
                return e.activation(out, in_, func, bias=bias, scale=scale, accum_out=accum_out)
            return e.activation(out, in_, func, bias=bias, scale=scale)
        SCH.add("act", fn, reads, writes)

    def ts(eng, out, in0, s1, s2, op0, op1, reads, writes):
        def fn(e, out=out, in0=in0, s1=s1, s2=s2, op0=op0, op1=op1):
            if op1 is None:
                return e.tensor_scalar(out, in0, s1, None, op0)
            return e.tensor_scalar(out, in0, s1, s2, op0, op1)
        SCH.add(eng, fn, reads, writes)

    def stt(eng, out, in0, scalar, in1, op0, op1, reads, writes):
        SCH.add(eng, lambda e, a=(out, in0, scalar, in1, op0, op1): e.scalar_tensor_tensor(*a), reads, writes)

    def tt(eng, out, in0, in1, op, reads, writes):
        SCH.add(eng, lambda e, a=(out, in0, in1, op): e.tensor_tensor(*a), reads, writes)

    def cp(eng, out, in_, reads, writes):
        if eng == "act":
            SCH.add(eng, lambda e, a=(out, in_): e.copy(*a), reads, writes)
        else:
            SCH.add(eng, lambda e, a=(out, in_): e.tensor_copy(*a), reads, writes)

    def memset(eng, ap, val, reads, writes):
        SCH.add(eng, lambda e, a=(ap, val): e.memset(*a), reads, writes)

    def recip(out, in_, reads, writes):
        SCH.add("dve", lambda e, a=(out, in_): e.reciprocal(*a), reads, writes)

    lam_init = [0.8 - 0.6 * math.exp(-0.3 * l) for l in range(2)]

    AR.reset()
    dma("sp", [(CST[:, :], cst_d[:, :])], [], ["CST"], "cst")
    cp("dve", CBF[:, :], CST[:, :], ["CST"], ["CBF"])
    PIN = AR.alloc([7, 128], F32)
    dma("sp", [(PIN[:, b, :], prm_d[b * 128:(b + 1) * 128, :]) for b in range(7)], [], ["PIN"], "pin")
    for b in range(7):
        bank = PS[b % 2]
        transp(bank[:, 0:128], PIN[:, b, :], ["PIN", "CST"], [("ps", b % 2)])
        cp("dve", PT[:, b * 128:(b + 1) * 128], bank[:, 0:128], [("ps", b % 2)], ["PT"])
    dma("sp", [(SG[:, :], sub_d[0, :].partition_broadcast(128))], [], ["SG"], "sg")
    ts("dve", GS[:, :], PT[:, 0:144], float(math.sqrt(D)), None, ALU.mult, None, ["PT"], ["GS"])
    for l in range(2):
        ts("dve", SG[:, l * 256:(l + 1) * 256], SG[:, l * 256:(l + 1) * 256], float(16.0 * (1.0 - lam_init[l])), None,
           ALU.mult, None, ["SG"], ["SG"])
    for l in range(2):
        ts("dve", QKG[:, l:l + 1], PT[:, P_QG + l:P_QG + l + 1], float(math.sqrt(128.0) * SCALE), None, ALU.mult, None,
           ["PT"], ["QKG"])
        ts("dve", QKG[:, 2 + l:3 + l], PT[:, P_KG + l:P_KG + l + 1], float(math.sqrt(128.0)), None, ALU.mult, None,
           ["PT"], ["QKG"])
    for l in range(2):
        for i in range(2):
            a = P_LAM + 4 * l + 2 * i
            tt("dve", LAM[:, 2 * l + i:2 * l + i + 1], PT[:, a:a + 1], PT[:, a + 1:a + 2], ALU.mult, ["PT"], ["LAM"])
    mm(PS[2][:, 0:4], [(ones_f, LAM[:, 0:4])], ["LAM", "CST"], [("ps", 2)])
    act(LAM[:, 4:8], PS[2][:, 0:4], AF.Exp, [("ps", 2)], ["LAM"])
    for l in range(2):
        tt("dve", LAM[:, 8 + l:9 + l], LAM[:, 4 + 2 * l:5 + 2 * l], LAM[:, 5 + 2 * l:6 + 2 * l], ALU.subtract,
           ["LAM"], ["LAM"])
        ts("dve", LAM[:, 10 + l:11 + l], LAM[:, 8 + l:9 + l], float(lam_init[l]), -1.0, ALU.add, ALU.mult,
           ["LAM"], ["LAM"])
    PI = math.pi
    for s in range(NSEQ):
        POSI = AR.alloc([S], I32)
        ANG = AR.alloc([S], F32)
        U = AR.alloc([2, S], F32)
        dma("sp", [(POSI[0:32, :], pos_d[s, :].partition_broadcast(32))], [], [("POSI", s)], "posi")
        cp("dve", ANG[0:32, :], POSI[0:32, :], [("POSI", s)], [("ANG", s)])
        ts("dve", ANG[0:32, :], ANG[0:32, :], CST[0:32, C_INVF:C_INVF + 1], None, ALU.mult, None,
           [("ANG", s), "CST"], [("ANG", s)])
        KI = AR.alloc([S], I32)
        KF = AR.alloc([S], F32)
        XS = AR.alloc([S], F32)
        MM = AR.alloc([S], F32)
        C1 = 6.28125
        C2 = 2 * PI - C1
        for i, sh in enumerate((0.5 * PI, 0.0)):
            ts("dve", XS[0:32, :], ANG[0:32, :], float(sh), None, ALU.add, None, [("ANG", s)], [("XS", s)])
            ts("dve", KI[0:32, :], XS[0:32, :], float(1.0 / (2 * PI)), None, ALU.mult, None, [("XS", s)], [("KI", s)])
            cp("dve", KF[0:32, :], KI[0:32, :], [("KI", s)], [("KF", s)])
            stt("dve", U[0:32, i, :], KF[0:32, :], float(-C1), XS[0:32, :], ALU.mult, ALU.add,
                [("KF", s), ("XS", s)], [("U", s, i)])
            stt("dve", U[0:32, i, :], KF[0:32, :], float(-C2), U[0:32, i, :], ALU.mult, ALU.add,
                [("KF", s), ("U", s, i)], [("U", s, i)])
            ts("dve", MM[0:32, :], U[0:32, i, :], float(PI), float(2 * PI), ALU.is_gt, ALU.mult, [("U", s, i)], [("MM", s)])
            tt("dve", U[0:32, i, :], U[0:32, i, :], MM[0:32, :], ALU.subtract, [("U", s, i), ("MM", s)], [("U", s, i)])
            ts("dve", U[0:32, i, :], U[0:32, i, :], 3.14159, -3.14159, ALU.min, ALU.max, [("U", s, i)], [("U", s, i)])
            act(U[0:32, i, :], U[0:32, i, :], AF.Sin, [("U", s, i)], [("U", s, i)])
        dma("sp", [(cs_d[s, i, :, :], U[0:32, i, :]) for i in range(2)], [("U", s, 0), ("U", s, 1)], [("cs", s)], "csst")
    SCH.barrier()

    wctr = [0]

    def run_jobs(jobs, W):
        NW = len(W)
        wj = [j for j in jobs]
        slots = []
        n = len(wj)
        PREF = NW - 1
        for i in range(n + PREF):
            j = i - PREF
            if j >= 0:
                wj[j][2](slots[j])
            if i < n:
                src, nk, _ = wj[i]
                sl = wctr[0] % NW
                wctr[0] += 1
                pairs = []
                for k0 in range(0, nk, 4):
                    k1 = min(nk, k0 + 4)
                    pairs.append((W[sl][:, k0:k1, :],
                                  src[k0 * 128:k1 * 128, :].rearrange("(k p) n -> p k n", p=128)))
                dma("pool", pairs, [], [("W", sl)], "W%d" % sl)
                slots.append(sl)

    def load_ht(HT, s, t0, q="sp"):
        pairs = [(HT[:, c0:c0 + 4, :], hT_d[s, c0:c0 + 4, :, t0:t0 + TT].rearrange("c p t -> p c t"))
                 for c0 in range(0, NCH, 4)]
        dma(q, pairs, [("hT", s, t0)], ["HT"], "HT")

    def store_ht(HT, s, t0, q="sp"):
        pairs = [(hT_d[s, c0:c0 + 4, :, t0:t0 + TT].rearrange("c p t -> p c t"), HT[:, c0:c0 + 4, :])
                 for c0 in range(0, NCH, 4)]
        dma(q, pairs, ["HT"], [("hT", s, t0)], "HTst")

    def rms_stats(HT, SQ, RS, psb):
        for c in range(NCH):
            sq = SQ[:, c % 2, :]
            act(sq, HT[:, c, :], AF.Square, ["HT"], [("SQ", c % 2)])
            mm(PS[psb][:, :], [(ones_b, sq)], [("SQ", c % 2), "CBF"], [("ps", psb)], start=(c == 0), stop=(c == NCH - 1))
        act(RS[:, :], PS[psb][:, :], AF.Ln, [("ps", psb)], ["RS"], bias=float(EPS * D))
        act(RS[:, :], RS[:, :], AF.Exp, ["RS"], ["RS"], scale=-0.5)

    def rms_apply(XN, xname, HT, RS, gbase):
        for c in range(NCH):
            stt("dve", XN[:, c, :], HT[:, c, :], GS[:, gbase + c:gbase + c + 1], RS[:, :], ALU.mult, ALU.mult,
                ["HT", "RS", "GS"], [xname])

    def phase_proj(l):
        AR.reset()
        HT = AR.alloc([NCH, TT], F32)
        XN = AR.alloc([NCH, TT], BF16)
        XN2 = AR.alloc([NCH, TT], BF16) if l == 2 else None
        W = [AR.alloc([NCH, 512], BF16) for _ in range(4)]
        SQ = AR.alloc([2, TT], BF16)
        RS = AR.alloc([TT], F32)
        RS2 = AR.alloc([2, TT], F32)
        QN = AR.alloc([3, TT], BF16)
        T1 = AR.alloc([2, TT], F32)
        T2 = AR.alloc([2, TT], F32)
        CS = AR.alloc([2, TT], F32)
        VST = AR.alloc([2, NQ, 512], BF16)
        XIN = AR.alloc([2, D], F32) if l == 0 else None
        ctr = {"qn": 0, "vst": 0, "pp": 0}
        jobs = []
        for s in range(NSEQ):
            for ti in range(NTT):
                t0 = ti * TT

                def prologue(s=s, t0=t0):
                    if l == 0:
                        for tq in range(NQ):
                            xi = tq % 2
                            dma("sp", [(XIN[:, xi, :], x_d[s, t0 + tq * 128:t0 + (tq + 1) * 128, :])], [],
                                [("XIN", xi)], "XIN%d" % xi)
                            for b in range(4):
                                pb = b % 2
                                for j in range(4):
                                    c = b * 4 + j
                                    transp(PS[pb][:, j * 128:(j + 1) * 128], XIN[:, xi, c * 128:(c + 1) * 128],
                                           [("XIN", xi), "CST"], [("ps", pb)])
                                cp("act" if b % 2 else "dve", HT[:, b * 4:(b + 1) * 4, tq * 128:(tq + 1) * 128],
                                   PS[pb][:, :].rearrange("p (a b) -> p a b", a=4), [("ps", pb)], ["HT"])
                        store_ht(HT, s, t0)
                    else:
                        load_ht(HT, s, t0)
                    rms_stats(HT, SQ, RS, 4)
                    rms_apply(XN, "XN", HT, RS, P_ATTN + l * 16)
                    if l == 2:
                        rms_apply(XN2, "XN2", HT, RS, P_KV)
                    if l < 2:
                        dma("sp", [(CS[0:32, i, :], cs_d[s, i, :, t0:t0 + TT]) for i in range(2)], [("cs", s)], ["CS"], "CS")

                def qk_group(sl, cg, wsel, s=s, t0=t0, first=False, prologue=prologue):
                    if first:
                        prologue()
                    Wv = W[sl]
                    xn = XN2 if wsel == "bk" else XN
                    xname = "XN2" if wsel == "bk" else "XN"
                    for j in range(4):
                        hc = cg * 4 + j
                        pq = ctr["pp"] % 4
                        pb = ctr["pp"] % 2
                        ctr["pp"] += 1
                        mm(PS[pq][:, :], [(Wv[:, k, j * 128:(j + 1) * 128], xn[:, k, :]) for k in range(NCH)],
                           [("W", sl), xname], [("ps", pq)])
                        qi = ctr["qn"] % 3
                        ctr["qn"] += 1
                        qn = QN[:, qi, :]
                        if wsel in ("aq", "ak"):
                            sq = SQ[:, pb, :]
                            act(sq, PS[pq][:, :], AF.Square, [("ps", pq)], [("SQ", pb)])
                            mm(PS[4 + pb][:, :], [(ones_b, sq)], [("SQ", pb), "CBF"], [("ps", 4 + pb)])
                            act(RS2[:, pb, :], PS[4 + pb][:, :], AF.Ln, [("ps", 4 + pb)], [("RS2", pb)], bias=float(EPS * 128))
                            act(RS2[:, pb, :], RS2[:, pb, :], AF.Exp, [("RS2", pb)], [("RS2", pb)], scale=-0.5)
                            gcol = (0 if wsel == "aq" else 2) + l
                            stt("dve", qn, PS[pq][:, :], QKG[:, gcol:gcol + 1], RS2[:, pb, :], ALU.mult, ALU.mult,
                                [("ps", pq), ("RS2", pb), "QKG"], [("QN", qi)])
                            mm(PS[6 + pb][0:32, :], [(rrot_b, qn[0:32, :])], [("QN", qi), "CBF"], [("ps", 6 + pb)])
                            tt("dve", T1[0:32, pb, :], PS[6 + pb][0:32, :], CS[0:32, 1, :], ALU.mult,
                               [("ps", 6 + pb), "CS"], [("T1", pb)])
                            tt("pool", T2[0:32, pb, :], qn[0:32, :], CS[0:32, 0, :], ALU.mult,
                               [("QN", qi), "CS"], [("T2", pb)])
                            tt("pool", qn[0:32, :], T1[0:32, pb, :], T2[0:32, pb, :], ALU.add,
                               [("T1", pb), ("T2", pb)], [("QN", qi)])
                        elif wsel == "bq":
                            SCH.add("act", lambda e, a=(qn, PS[pq][:, :], float(SCALE)): e.mul(*a), [("ps", pq)], [("QN", qi)])
                        else:
                            cp("act", qn, PS[pq][:, :], [("ps", pq)], [("QN", qi)])
                        dst = kT_d if wsel in ("ak", "bk") else qT_d
                        dma("sp", [(dst[s, hc, :, t0:t0 + TT], qn)], [("QN", qi)], [("qk", wsel, s, hc, t0)], "QN%d" % qi)

                def v_group(sl, cg, xsel, s=s, t0=t0):
                    Wv = W[sl]
                    xn = XN2 if xsel == "XN2" else XN
                    vi = ctr["vst"] % 2
                    ctr["vst"] += 1
                    for tq in range(NQ):
                        pb = ctr["pp"] % 4
                        ctr["pp"] += 1
                        mm(PS[pb][:, :], [(xn[:, k, tq * 128:(tq + 1) * 128], Wv[:, k, :]) for k in range(NCH)],
                           [("W", sl), xsel], [("ps", pb)])
                        cp("act", VST[:, vi, tq, :], PS[pb][:, :], [("ps", pb)], [("VST", vi)])
                    dma("sp", [(v_d[s, t0:t0 + TT, cg * 512:(cg + 1) * 512].rearrange("(q p) e -> p q e", p=128),
                                VST[:, vi, :, :])], [("VST", vi)], [("v", s, t0, cg)], "VST%d" % vi)

                if l < 2:
                    for cg in range(8):
                        wsel = "aq" if cg < 4 else "ak"
                        jobs.append((wqkv_d[l, :, cg * 512:(cg + 1) * 512], NCH,
                                     lambda sl, cg=cg, wsel=wsel, f=qk_group, first=(cg == 0): f(sl, cg % 4, wsel, first=first)))
                    for cg in range(4):
                        jobs.append((wqkv_d[l, :, (8 + cg) * 512:(9 + cg) * 512], NCH,
                                     lambda sl, cg=cg, f=v_group: f(sl, cg, "XN")))
                else:
                    jb = l - 2
                    for cg in range(4):
                        jobs.append((bwq_d[jb, :, cg * 512:(cg + 1) * 512], NCH,
                                     lambda sl, cg=cg, f=qk_group, first=(cg == 0): f(sl, cg, "bq", first=first)))
                    if l == 2:
                        for cg in range(4):
                            jobs.append((bkv_d[:, cg * 512:(cg + 1) * 512], NCH,
                                         lambda sl, cg=cg, f=qk_group: f(sl, cg, "bk")))
                        for cg in range(4):
                            jobs.append((bkv_d[:, (4 + cg) * 512:(5 + cg) * 512], NCH,
                                         lambda sl, cg=cg, f=v_group: f(sl, cg, "XN2")))
        run_jobs(jobs, W)
        SCH.barrier()

    def tri_off(kt):
        return kt * S - 128 * (kt * (kt - 1) // 2)

    TRI = tri_off(NT)

    def chunks(lo, hi):
        c = []
        while lo < hi:
            n = min(512, hi - lo)
            c.append((lo, n))
            lo += n
        return c

    def phase_attn_a(l):
        AR.reset()
        QK = [AR.alloc([4, S], BF16) for _ in range(2)]
        VE = [AR.alloc([NT, 258], BF16) for _ in range(2)]
        PTS = [AR.alloc([TRI], BF16) for _ in range(2)]
        RR = AR.alloc([4, 4], F32)
        T1 = AR.alloc([4, 256], F32)
        OO = AR.alloc([4, 256], F32)
        OF = AR.alloc([4, 256], F32)
        OS = AR.alloc([4, 2, 258], F32)
        JK = AR.alloc([256], F32)
        OTS = AR.alloc([2, 2, TT], BF16)
        for i in range(2):
            memset("pool", VE[i][:, :, 256:258], 1.0, [], [("VE", i)])
        hi = 0
        pp = 0
        fin = 0
        for s in range(NSEQ):
            for h in range(8):
                b = hi % 2
                hi += 1
                dma("sp", [(QK[b][:, 0, :], qT_d[s, 2 * h, :, :]), (QK[b][:, 1, :], qT_d[s, 2 * h + 1, :, :]),
                           (QK[b][:, 2, :], kT_d[s, 2 * h, :, :]), (QK[b][:, 3, :], kT_d[s, 2 * h + 1, :, :])],
                    [], [("QK", b)], "QK%d" % b)
                vp = []
                for k0 in range(0, NT, 4):
                    k1 = min(NT, k0 + 4)
                    vp.append((VE[b][:, k0:k1, 0:256],
                               v_d[s, k0 * 128:k1 * 128, h * 256:(h + 1) * 256].rearrange("(t p) e -> p t e", p=128)))
                dma("sp", vp, [], [("VE", b)], "VE%d" % b)
                for c in range(2):
                    for kt in range(NT):
                        for (q0, n) in chunks(kt * 128, S):
                            pb = pp % 2
                            pp += 1
                            mm(PS[pb][:, 0:n], [(QK[b][:, 2 + c, kt * 128:(kt + 1) * 128], QK[b][:, c, q0:q0 + n])],
                               [("QK", b)], [("ps", pb)])
                            o = tri_off(kt) + (q0 - kt * 128)
                            act(PTS[c][:, o:o + n], PS[pb][:, 0:n], AF.Exp, [("ps", pb)], [("PTS", c, kt)])
                        o = tri_off(kt)
                        memset("pool", PTS[c][64:128, o:o + 64], 0.0, [], [("PTS", c, kt)])
                for qt in range(NT):
                    pf = fin % 2
                    f = fin % 4
                    fin += 1
                    for c in range(2):
                        pb = 2 + 2 * pf + c
                        mm(PS[pb][:, 0:257],
                           [(PTS[c][:, tri_off(kt) + (qt - kt) * 128: tri_off(kt) + (qt - kt + 1) * 128],
                             VE[b][:, kt, 0:257]) for kt in range(qt + 1)],
                           [("PTS", c, kt) for kt in range(qt + 1)] + [("VE", b)], [("ps", pb)])
                        cp("dve", OS[:, f, c, 0:257], PS[pb][:, 0:257], [("ps", pb)], [("OS", f, c)])
                    p0, p1 = OS[:, f, 0, :], OS[:, f, 1, :]
                    recip(RR[:, f, 0:1], p0[:, 256:257], [("OS", f, 0)], [("RR", f)])
                    recip(RR[:, f, 1:2], p1[:, 256:257], [("OS", f, 1)], [("RR", f)])
                    tt("dve", RR[:, f, 2:3], RR[:, f, 1:2], LAM[:, 10 + l:11 + l], ALU.mult, [("RR", f), "LAM"], [("RR", f)])
                    ts("dve", T1[:, f, :], p1[:, 0:256], RR[:, f, 2:3], None, ALU.mult, None,
                       [("OS", f, 1), ("RR", f)], [("T1", f)])
                    stt("dve", OO[:, f, :], p0[:, 0:256], RR[:, f, 0:1], T1[:, f, :], ALU.mult, ALU.add,
                        [("OS", f, 0), ("RR", f), ("T1", f)], [("OO", f)])
                    memset("pool", RR[:, f, 3:4], 0.0, [], [("RS3", f)])
                    act(JK[:, :], OO[:, f, :], AF.Square, [("OO", f), ("RS3", f)], ["JK", ("RS3", f)],
                        accum_out=RR[:, f, 3:4])
                    act(RR[:, f, 3:4], RR[:, f, 3:4], AF.Ln, [("RS3", f)], [("RS3", f)], bias=float(EPS * 256))
                    act(RR[:, f, 3:4], RR[:, f, 3:4], AF.Exp, [("RS3", f)], [("RS3", f)], scale=-0.5)
                    stt("dve", OF[:, f, :], OO[:, f, :], RR[:, f, 3:4], SG[:, l * 256:(l + 1) * 256], ALU.mult, ALU.mult,
                        [("OO", f), ("RS3", f), "SG"], [("OF", f)])
                    tb = 6 + pf
                    for i in range(2):
                        transp(PS[tb][:, i * 128:(i + 1) * 128], OF[:, f, i * 128:(i + 1) * 128], [("OF", f), "CST"],
                               [("ps", tb)])
                    g = qt // NQ
                    ob = g % 2
                    cp("act", OTS[:, ob, :, (qt % NQ) * 128:(qt % NQ + 1) * 128],
                       PS[tb][:, 0:256].rearrange("p (a b) -> p a b", a=2), [("ps", tb)], [("OTS", ob)])
                    if qt % NQ == NQ - 1:
                        dma("sp", [(oT_d[s, 2 * h:2 * h + 2, :, g * TT:(g + 1) * TT].rearrange("c p t -> p c t"),
                                    OTS[:, ob, :, :])], [("OTS", ob)], [("oT", s, h, g)], "OTS%d" % ob)
        SCH.barrier()

    def phase_attn_b():
        AR.reset()
        QB = [AR.alloc([2, S], BF16) for _ in range(2)]
        VB = [AR.alloc([NT, 128], BF16) for _ in range(2)]
        SP = AR.alloc([TRI], BF16)
        SS = AR.alloc([TRI], BF16)
        ACC = AR.alloc([S], F32)
        EE = AR.alloc([2, 512], F32)
        AT = AR.alloc([3, 512], BF16)
        OTS = AR.alloc([2, TT], BF16)
        hi = 0
        pp = 0
        lp = 0
        ai = 0
        oi = 0
        for s in range(NSEQ):
            for h in range(16):
                b = hi % 2
                hi += 1
                dma("sp", [(QB[b][:, 0, :], qT_d[s, h, :, :]), (QB[b][:, 1, :], kT_d[s, h, :, :])],
                    [], [("QB", b)], "QB%d" % b)
                vp = []
                for k0 in range(0, NT, 4):
                    k1 = min(NT, k0 + 4)
                    vp.append((VB[b][:, k0:k1, :],
                               v_d[s, k0 * 128:k1 * 128, h * 128:(h + 1) * 128].rearrange("(t p) e -> p t e", p=128)))
                dma("sp", vp, [], [("VB", b)], "VB%d" % b)
                for kt in range(NT):
                    for (q0, n) in chunks(kt * 128, S):
                        pb = pp % 2
                        pp += 1
                        mm(PS[pb][:, 0:n], [(QB[b][:, 1, kt * 128:(kt + 1) * 128], QB[b][:, 0, q0:q0 + n])],
                           [("QB", b)], [("ps", pb)])
                        act(EE[:, pb, 0:n], PS[pb][:, 0:n], AF.Exp, [("ps", pb)], [("EE", pb)])
                        o = tri_off(kt) + (q0 - kt * 128)
                        act(SP[:, o:o + n], EE[:, pb, 0:n], AF.Ln, [("EE", pb)], [("SP", kt)], bias=1.0)
                    o = tri_off(kt)
                    tt("dve", SP[:, o:o + 128], SP[:, o:o + 128], maskb_b, ALU.mult, [("SP", kt), "CBF"], [("SP", kt)])
                memset("dve", ACC[:, :], 0.0, [], ["ACC"])
                for kt in range(NT - 2, -1, -1):
                    lo = (kt + 1) * 128
                    o1 = tri_off(kt + 1)
                    tt("dve", ACC[:, lo:S], ACC[:, lo:S], SP[:, o1:o1 + (S - lo)], ALU.add, ["ACC", ("SP", kt + 1)], ["ACC"])
                    o = tri_off(kt)
                    cp("dve", SS[:, o + 128:o + 128 + (S - lo)], ACC[:, lo:S], ["ACC"], [("SS", kt)])
                for Q in range(S // 512):
                    ob = 4 + (oi % 2)
                    oi += 1
                    ktmax = min(NT - 1, 4 * Q + 3)
                    for kt in range(ktmax + 1):
                        q0 = max(Q * 512, kt * 128)
                        n = (Q + 1) * 512 - q0
                        lo = q0 - Q * 512
                        lb = 2 + (lp % 2)
                        lp += 1
                        o = tri_off(kt) + (q0 - kt * 128)
                        pairs = [(QB[b][:, 1, kt * 128:(kt + 1) * 128], QB[b][:, 0, q0:q0 + n]),
                                 (ntri_b, SP[:, o:o + n])]
                        rds = [("QB", b), ("SP", kt), "CBF"]
                        q1 = max(q0, (kt + 1) * 128)
                        has_ss = q1 < (Q + 1) * 512 and kt < NT - 1
                        mm(PS[lb][:, lo:512], pairs, rds, [("ps", lb)], start=True, stop=not has_ss)
                        if has_ss:
                            o2 = tri_off(kt) + (q1 - kt * 128)
                            mm(PS[lb][:, q1 - Q * 512:512], [(nones_b, SS[:, o2:o2 + (Q + 1) * 512 - q1])],
                               [("SS", kt), "CBF"], [("ps", lb)], start=False, stop=True)
                        a = ai % 3
                        ai += 1
                        act(AT[:, a, 0:n], PS[lb][:, lo:512], AF.Exp, [("ps", lb)], [("AT", a)])
                        if kt * 128 >= Q * 512:
                            tt("dve", AT[:, a, 0:128], AT[:, a, 0:128], maskb_b, ALU.mult, [("AT", a), "CBF"], [("AT", a)])
                        mm(PS[ob][:, lo:512], [(VB[b][:, kt, :], AT[:, a, 0:n])], [("VB", b), ("AT", a)], [("ps", ob)],
                           start=(kt == 0), stop=(kt == ktmax))
                    sb_ = oi % 2
                    cp("act", OTS[:, sb_, :], PS[ob][:, :], [("ps", ob)], [("OTS", sb_)])
                    dma("sp", [(oT_d[s, h, :, Q * 512:(Q + 1) * 512], OTS[:, sb_, :])], [("OTS", sb_)], [("oT", s, h, Q)],
                        "OTS%d" % sb_)
        SCH.barrier()

    def phase_ffn(l, last):
        AR.reset()
        HT = AR.alloc([NCH, TT], F32)
        XN = AR.alloc([NCH, TT], BF16)
        W = [AR.alloc([NCH, 512], BF16) for _ in range(3)]
        ACTB = AR.alloc([NFF, TT], BF16)
        SQ = AR.alloc([2, TT], BF16)
        RS = AR.alloc([TT], F32)
        GE = AR.alloc([2, TT + 2], F32)
        A1 = AR.alloc([2, TT], F32)
        A2 = AR.alloc([2, TT], F32)
        UU = AR.alloc([2, TT], F32)
        XO = AR.alloc([D], F32) if last else None
        wo_src = awo_d[l] if l < 2 else bwo_d[l - 2]
        cwb = P_CW + l * 3 * NFF
        cbb = P_CB + l * NFF
        ctr = {"pp": 0, "ge": 0}
        jobs = []
        for s in range(NSEQ):
            for ti in range(NTT):
                t0 = ti * TT

                def prologue(s=s, t0=t0):
                    load_ht(HT, s, t0)
                    dma("sp", [(XN[:, c0:c0 + 4, :], oT_d[s, c0:c0 + 4, :, t0:t0 + TT].rearrange("c p t -> p c t"))
                               for c0 in range(0, NCH, 4)], [], ["XN"], "XN")

                def oproj(sl, cg, first, s=s, t0=t0, ti=ti, prologue=prologue):
                    if first:
                        prologue()
                    for j in range(4):
                        oc = cg * 4 + j
                        pb = ctr["pp"] % 2
                        ctr["pp"] += 1
                        mm(PS[pb][:, :], [(W[sl][:, k, j * 128:(j + 1) * 128], XN[:, k, :]) for k in range(NCH)],
                           [("W", sl), "XN"], [("ps", pb)])
                        tt("dve", HT[:, oc, :], HT[:, oc, :], PS[pb][:, :], ALU.add, ["HT", ("ps", pb)], ["HT"])
                    if cg == 3:
                        rms_stats(HT, SQ, RS, 2)
                        rms_apply(XN, "XN", HT, RS, P_FFN + l * 16)

                def up(slu, slg, gi, s=s, t0=t0, ti=ti):
                    for j in range(4):
                        c = gi * 4 + j
                        pb = ctr["pp"] % 2
                        ctr["pp"] += 1
                        pu, pg = PS[pb], PS[2 + pb]
                        mm(pu[:, :], [(W[slu][:, k, j * 128:(j + 1) * 128], XN[:, k, :]) for k in range(NCH)],
                           [("W", slu), "XN"], [("ps", pb)])
                        mm(pg[:, :], [(W[slg][:, k, j * 128:(j + 1) * 128], XN[:, k, :]) for k in range(NCH)],
                           [("W", slg), "XN"], [("ps", 2 + pb)])
                        gi_ = ctr["ge"] % 2
                        ctr["ge"] += 1
                        ge = GE[:, gi_, :]
                        cp("act", UU[:, gi_, :], pu[:, :], [("ps", pb)], [("UU", gi_)])
                        if ti == 0:
                            memset("pool", ge[:, 0:2], 0.0, [], [("GE", gi_)])
                        else:
                            cp("pool", ge[:, 0:2], HALO[:, c, :], [("HALO", c)], [("GE", gi_)])
                        cp("act", ge[:, 2:TT + 2], pg[:, :], [("ps", 2 + pb)], [("GE", gi_)])
                        cp("pool", HALO[:, c, :], ge[:, TT:TT + 2], [("GE", gi_)], [("HALO", c)])
                        w0 = PT[:, cwb + 0 * NFF + c:cwb + 0 * NFF + c + 1]
                        w1 = PT[:, cwb + 1 * NFF + c:cwb + 1 * NFF + c + 1]
                        w2 = PT[:, cwb + 2 * NFF + c:cwb + 2 * NFF + c + 1]
                        bb = PT[:, cbb + c:cbb + c + 1]
                        ts("dve", A1[:, gi_, :], ge[:, 0:TT], w0, bb, ALU.mult, ALU.add, [("GE", gi_), "PT"], [("A1", gi_)])
                        stt("dve", A2[:, gi_, :], ge[:, 1:TT + 1], w1, A1[:, gi_, :], ALU.mult, ALU.add,
                            [("GE", gi_), ("A1", gi_), "PT"], [("A2", gi_)])
                        stt("dve", A1[:, gi_, :], ge[:, 2:TT + 2], w2, A2[:, gi_, :], ALU.mult, ALU.add,
                            [("GE", gi_), ("A2", gi_), "PT"], [("A1", gi_)])
                        act(A2[:, gi_, :], A1[:, gi_, :], AF.Silu, [("A1", gi_)], [("A2", gi_)])
                        tt("dve", ACTB[:, c, :], A2[:, gi_, :], UU[:, gi_, :], ALU.mult, [("A2", gi_), ("UU", gi_)], [("ACTB", c)])

                def down(sl, cg, kp, s=s, t0=t0, ti=ti):
                    for j in range(4):
                        pd = 4 + j
                        mm(PS[pd][:, :], [(W[sl][:, kk, j * 128:(j + 1) * 128], ACTB[:, kp * 11 + kk, :]) for kk in range(11)],
                           [("W", sl)] + [("ACTB", kp * 11 + kk) for kk in range(11)], [("ps", pd)],
                           start=(kp == 0), stop=(kp == 3))
                    if kp == 3:
                        for j in range(4):
                            oc = cg * 4 + j
                            tt("dve", HT[:, oc, :], HT[:, oc, :], PS[4 + j][:, :], ALU.add, ["HT", ("ps", 4 + j)], ["HT"])
                        if cg == 3:
                            if not last:
                                store_ht(HT, s, t0)
                            else:
                                for tq in range(NQ):
                                    for b4 in range(4):
                                        pb = b4 % 2
                                        for j in range(4):
                                            c = b4 * 4 + j
                                            transp(PS[pb][:, j * 128:(j + 1) * 128], HT[:, c, tq * 128:(tq + 1) * 128],
                                                   ["HT", "CST"], [("ps", pb)])
                                        cp("act" if b4 % 2 else "dve", XO[:, b4 * 512:(b4 + 1) * 512], PS[pb][:, :],
                                           [("ps", pb)], ["XO"])
                                    dma("sp", [(out_d[s, t0 + tq * 128:t0 + (tq + 1) * 128, :], XO[:, :])], ["XO"],
                                        [("out", s, t0, tq)], "XO")

                for cg in range(4):
                    jobs.append((wo_src[:, cg * 512:(cg + 1) * 512], NCH,
                                 lambda sl, cg=cg, f=oproj: f(sl, cg, cg == 0)))
                for gi in range(11):
                    holder = {}
                    jobs.append((wup_d[l, :, gi * 512:(gi + 1) * 512], NCH,
                                 lambda sl, holder=holder: holder.__setitem__("u", sl)))
                    jobs.append((wup_d[l, :, DFF + gi * 512:DFF + (gi + 1) * 512], NCH,
                                 lambda sl, gi=gi, holder=holder, f=up: f(holder["u"], sl, gi)))
                for cg in range(4):
                    for kp in range(4):
                        jobs.append((wdn_d[l, kp * 1408:(kp + 1) * 1408, cg * 512:(cg + 1) * 512], 11,
                                     lambda sl, cg=cg, kp=kp, f=down: f(sl, cg, kp)))
        run_jobs(jobs, W)
        SCH.barrier()

    for l in range(n_layers):
        phase_proj(l)
        if l < 2:
            phase_attn_a(l)
        else:
            phase_attn_b()
        phase_ffn(l, last=(l == n_layers - 1))

    with ExitStack() as stack:
        SCH.emit(nc, stack)
    return nc, SCH


W_NAMES = ["a_w_qkv", "a_w_o", "b_w_kv", "b_w_q", "b_w_o", "ffn_w_up", "ffn_w_down"]


def make_in_maps(inputs, n_cores, nseq, S):
    f = lambda a: np.ascontiguousarray(np.asarray(a), dtype=np.float32)
    cst = make_consts()
    prm = pack_params(f(inputs["attn_norm_g"]), f(inputs["ffn_norm_g"]), f(inputs["kv_norm_g"]),
                      f(inputs["ffn_conv_w"]), f(inputs["ffn_conv_b"]), f(inputs["a_q_norm_g"]),
                      f(inputs["a_k_norm_g"]), f(inputs["a_lambda_q1"]), f(inputs["a_lambda_k1"]),
                      f(inputs["a_lambda_q2"]), f(inputs["a_lambda_k2"]))
    sub = f(inputs["a_subln_g"]).reshape(1, 512)
    ws = {k: f(inputs[k]) for k in W_NAMES}
    x = np.asarray(inputs["x"])
    pos = np.asarray(inputs["positions"]).astype(np.int32)
    maps = []
    for c in range(n_cores):
        m = {"x": np.ascontiguousarray(x[c * nseq:(c + 1) * nseq, :S], dtype=np.float32),
             "pos": np.ascontiguousarray(pos[c * nseq:(c + 1) * nseq, :S]),
             "cst": cst, "prm": prm, "subln": sub}
        m.update(ws)
        maps.append(m)
    return maps


def kernel(**inputs):
    n_cores, nseq, S = 8, 2, 2048
    nc, _ = build(S=S, NSEQ=nseq)
    maps = make_in_maps(inputs, n_cores, nseq, S)
    res = run_bass_kernel_spmd(nc, maps, core_ids=list(range(n_cores)))
    return np.concatenate([np.asarray(r["out"], dtype=np.float32) for r in res.results], axis=0)
```

```python
import math
from contextlib import ExitStack

import numpy as np
import concourse.bass as bass
import concourse.mybir as mybir
from concourse.bass_utils import run_bass_kernel_spmd

F32 = mybir.dt.float32
BF16 = mybir.dt.bfloat16
I32 = mybir.dt.int32
AF = mybir.ActivationFunctionType
ALU = mybir.AluOpType
AX = mybir.AxisListType

D = 2048
NCH = 16
DFF = 5632
NFF = 44
TT = 512
EPS = 1e-6
SCALE = 128 ** -0.5
SEM_EPOCH = 30000


class Op:
    __slots__ = ("eng", "fn", "deps", "ndma", "key", "sig", "awaited", "idx", "join")


class Sched:
    def __init__(self):
        self.ops = []
        self.last_w = {}
        self.readers = {}
        self.last_op = {}

    def barrier(self):
        deps = sorted(set(self.last_op.values()))
        for e in ("pe", "act", "dve", "pool", "sp"):
            idx = self.add(e, lambda eh: None, join=True)
            op = self.ops[idx]
            op.deps = list(deps)
            for d in deps:
                self.ops[d].awaited = True
        self.last_w = {}
        self.readers = {}

    def add(self, eng, fn, reads=(), writes=(), ndma=0, key=None, join=False):
        idx = len(self.ops)
        deps = set()
        for r in reads:
            w = self.last_w.get(r)
            if w is not None:
                deps.add(w)
        for r in writes:
            w = self.last_w.get(r)
            if w is not None:
                deps.add(w)
            for rd in self.readers.get(r, ()):
                deps.add(rd)
        for r in reads:
            self.readers.setdefault(r, []).append(idx)
        for r in writes:
            self.last_w[r] = idx
            self.readers[r] = []
        op = Op()
        op.eng, op.fn, op.ndma, op.key, op.idx = eng, fn, ndma, key, idx
        op.awaited = False
        op.sig = None
        op.join = join
        if not join:
            self.last_op[("d", key) if ndma else ("e", eng)] = idx
        if ndma == 0 and eng == "pe":
            deps = {d for d in deps if not (self.ops[d].eng == "pe" and self.ops[d].ndma == 0)}
        deps.discard(idx)
        op.deps = sorted(deps)
        for d in op.deps:
            self.ops[d].awaited = True
        self.ops.append(op)
        return idx

    def emit(self, nc, stack):
        engs = ["pe", "act", "dve", "pool", "sp"]
        sems = {}

        def get_sem(name):
            if name not in sems:
                sems[name] = stack.enter_context(nc.semaphore(name))
            return sems[name]

        cnt = {e: [0, 0] for e in engs}
        dcnt = {}
        for op in self.ops:
            if op.ndma:
                st = dcnt.setdefault(op.key, [0, 0])
                inc = 16 * op.ndma
                if st[1] + inc > SEM_EPOCH:
                    st[0] += 1
                    st[1] = 0
                st[1] += inc
                op.sig = ("d_%s_%d" % (op.key, st[0]), st[1])
            elif op.awaited:
                st = cnt[op.eng]
                if st[1] + 1 > SEM_EPOCH:
                    st[0] += 1
                    st[1] = 0
                st[1] += 1
                op.sig = ("e_%s_%d" % (op.eng, st[0]), st[1])
        for op in self.ops:
            if op.sig is not None:
                get_sem(op.sig[0])
        self.nsems = len(sems)
        per_eng = {e: [op for op in self.ops if op.eng == e] for e in engs}
        ops = self.ops

        def run_engine(ename, eh):
            waited = {}
            for op in per_eng[ename]:
                need = {}
                for d in op.deps:
                    s = ops[d].sig
                    if need.get(s[0], 0) < s[1]:
                        need[s[0]] = s[1]
                for sname, v in need.items():
                    if waited.get(sname, 0) >= v:
                        continue
                    eh.wait_ge(sems[sname], v)
                    waited[sname] = v
                r = op.fn(eh)
                if r is None:
                    assert op.sig is None
                    continue
                if op.ndma:
                    sem = sems[op.sig[0]]
                    assert len(r) == op.ndma
                    for ins in r:
                        ins.then_inc(sem, 16)
                elif op.sig is not None:
                    r.then_inc(sems[op.sig[0]], 1)

        with nc.Block() as block:
            @block.tensor
            def _(e):
                run_engine("pe", e)

            @block.scalar
            def _(e):
                run_engine("act", e)

            @block.vector
            def _(e):
                run_engine("dve", e)

            @block.gpsimd
            def _(e):
                run_engine("pool", e)

            @block.sync
            def _(e):
                run_engine("sp", e)


C_ID, C_ONES, C_NTRI, C_NONES, C_MASKB, C_RROT, C_INVF, NCST = 0, 128, 256, 384, 512, 640, 672, 680
P_ATTN, P_FFN, P_KV, P_CW, P_CB, P_QG, P_KG, P_LAM, NPROW = 0, 64, 128, 144, 672, 848, 850, 852, 896


def make_consts():
    c = np.zeros((128, NCST), np.float32)
    c[:, C_ID:C_ID + 128] = np.eye(128, dtype=np.float32)
    c[:, C_ONES:C_ONES + 128] = 1.0
    kk = np.arange(128)
    c[:, C_NTRI:C_NTRI + 128] = -(kk[:, None] >= kk[None, :]).astype(np.float32)
    c[:, C_NONES:C_NONES + 128] = -1.0
    c[:, C_MASKB:C_MASKB + 128] = (kk[None, :] > kk[:, None]).astype(np.float32)
    r = np.zeros((128, 32), np.float32)
    for d in range(16):
        r[d + 16, d] = -1.0
        r[d, d + 16] = 1.0
    c[:, C_RROT:C_RROT + 32] = r
    inv = (np.float32(500000.0) ** (-np.arange(0, 32, 2, dtype=np.float32) / np.float32(32))).astype(np.float32)
    c[0:16, C_INVF] = inv
    c[16:32, C_INVF] = inv
    return c


def pack_params(attn_norm_g, ffn_norm_g, kv_norm_g, ffn_conv_w, ffn_conv_b, a_q_norm_g, a_k_norm_g,
                lq1, lk1, lq2, lk2):
    p = np.zeros((NPROW, 128), np.float32)
    p[P_ATTN:P_ATTN + 64] = attn_norm_g.reshape(64, 128)
    p[P_FFN:P_FFN + 64] = ffn_norm_g.reshape(64, 128)
    p[P_KV:P_KV + 16] = kv_norm_g.reshape(16, 128)
    p[P_CW:P_CW + 528] = ffn_conv_w.reshape(4 * 3 * NFF, 128)
    p[P_CB:P_CB + 176] = ffn_conv_b.reshape(4 * NFF, 128)
    p[P_QG:P_QG + 2] = a_q_norm_g
    p[P_KG:P_KG + 2] = a_k_norm_g
    for l in range(2):
        p[P_LAM + 4 * l + 0] = lq1[l]
        p[P_LAM + 4 * l + 1] = lk1[l]
        p[P_LAM + 4 * l + 2] = lq2[l]
        p[P_LAM + 4 * l + 3] = lk2[l]
    return p


def build(S=2048, NSEQ=2, n_layers=4, dbg=False):
    nc = bass.Bass("TRN2", target_bir_lowering=False)
    NT = S // 128
    NTT = S // TT
    NQ = TT // 128
    SCH = Sched()
    kind_dbg = "ExternalOutput" if dbg else "Internal"

    x_d = nc.dram_tensor("x", [NSEQ, S, D], F32, kind="ExternalInput")
    pos_d = nc.dram_tensor("pos", [NSEQ, S], I32, kind="ExternalInput")
    cst_d = nc.dram_tensor("cst", [128, NCST], F32, kind="ExternalInput")
    prm_d = nc.dram_tensor("prm", [NPROW, 128], F32, kind="ExternalInput")
    sub_d = nc.dram_tensor("subln", [1, 512], F32, kind="ExternalInput")
    wqkv_d = nc.dram_tensor("a_w_qkv", [2, D, 3 * D], F32, kind="ExternalInput")
    awo_d = nc.dram_tensor("a_w_o", [2, D, D], F32, kind="ExternalInput")
    bkv_d = nc.dram_tensor("b_w_kv", [D, 2 * D], F32, kind="ExternalInput")
    bwq_d = nc.dram_tensor("b_w_q", [2, D, D], F32, kind="ExternalInput")
    bwo_d = nc.dram_tensor("b_w_o", [2, D, D], F32, kind="ExternalInput")
    wup_d = nc.dram_tensor("ffn_w_up", [4, D, 2 * DFF], F32, kind="ExternalInput")
    wdn_d = nc.dram_tensor("ffn_w_down", [4, DFF, D], F32, kind="ExternalInput")
    out_d = nc.dram_tensor("out", [NSEQ, S, D], F32, kind="ExternalOutput")

    hT_d = nc.dram_tensor("hT", [NSEQ, NCH, 128, S], F32, kind=kind_dbg)
    cs_d = nc.dram_tensor("cs", [NSEQ, 2, 32, S], F32, kind=kind_dbg)
    qT_d = nc.dram_tensor("qT", [NSEQ, NCH, 128, S], BF16, kind=kind_dbg)
    kT_d = nc.dram_tensor("kT", [NSEQ, NCH, 128, S], BF16, kind=kind_dbg)
    v_d = nc.dram_tensor("v", [NSEQ, S, D], BF16, kind=kind_dbg)
    oT_d = nc.dram_tensor("oT", [NSEQ, NCH, 128, S], BF16, kind=kind_dbg)

    def sb(name, shape, dt):
        return nc.alloc_sbuf_tensor(name, [128] + list(shape), dt)

    CST = sb("CST", [NCST], F32)
    CBF = sb("CBF", [NCST], BF16)
    PT = sb("PT", [NPROW], F32)
    GS = sb("GS", [144], F32)
    SG = sb("SG", [512], F32)
    LAM = sb("LAM", [16], F32)
    QKG = sb("QKG", [4], F32)
    HALO = sb("HALO", [NFF, 2], F32)
    ARENA_WORDS = 44000
    ARENA = sb("ARENA", [ARENA_WORDS], F32)
    PS = [nc.alloc_psum_tensor("ps%d" % i, [128, 512], F32) for i in range(8)]

    ident = CST[:, C_ID:C_ID + 128]
    ones_f = CST[:, C_ONES:C_ONES + 128]
    ones_b = CBF[:, C_ONES:C_ONES + 128]
    ntri_b = CBF[:, C_NTRI:C_NTRI + 128]
    nones_b = CBF[:, C_NONES:C_NONES + 128]
    maskb_b = CBF[:, C_MASKB:C_MASKB + 128]
    rrot_b = CBF[0:32, C_RROT:C_RROT + 32]

    class Arena:
        def __init__(self):
            self.off = 0

        def reset(self):
            self.off = 0

        def alloc(self, shape, dt):
            n = int(np.prod(shape))
            words = (n * (2 if dt == BF16 else 4) + 3) // 4
            words = (words + 7) // 8 * 8
            assert self.off + words <= ARENA_WORDS, ("arena overflow", self.off, words)
            v = ARENA[:, self.off:self.off + words]
            self.off += words
            if dt == BF16:
                v = v.bitcast(BF16)
            elif dt == I32:
                v = v.bitcast(I32)
            v = v[:, 0:n]
            if len(shape) == 2:
                v = v.rearrange("p (a b) -> p a b", a=shape[0])
            elif len(shape) == 3:
                v = v.rearrange("p (a b c) -> p a b c", a=shape[0], b=shape[1])
            return v

    AR = Arena()

    def dma(q, pairs, reads, writes, key):
        def fn(e, pairs=pairs):
            return [e.dma_start(out=d, in_=s) for d, s in pairs]
        SCH.add(q, fn, reads, writes, ndma=len(pairs), key=key)

    def mm(ps, pairs, reads, writes, start=True, stop=True):
        def fn(e, ps=ps, pairs=pairs, start=start, stop=stop):
            n = len(pairs)
            ins = None
            for i, (l, r) in enumerate(pairs):
                ins = e.matmul(ps, lhsT=l, rhs=r, start=(start and i == 0), stop=(stop and i == n - 1))
            return ins
        SCH.add("pe", fn, reads, writes)

    def transp(ps, src, reads, writes):
        SCH.add("pe", lambda e, ps=ps, src=src: e.transpose(ps, src, ident), reads, writes)

    def act(out, in_, func, reads, writes, bias=0.0, scale=1.0, accum_out=None):
        def fn(e, out=out, in_=in_, func=func, bias=bias, scale=scale, accum_out=accum_out):
            if accum_out is not None:
                return e.activation(out, in_, func, bias=bias, scale=scale, accum_out=accum_out)
            return e.activation(out, in_, func, bias=bias, scale=scale)
        SCH.add("act", fn, reads, writes)

    def ts(eng, out, in0, s1, s2, op0, op1, reads, writes):
        def fn(e, out=out, in0=in0, s1=s1, s2=s2, op0=op0, op1=op1):
            if op1 is None:
                return e.tensor_scalar(out, in0, s1, None, op0)
            return e.tensor_scalar(out, in0, s1, s2, op0, op1)
        SCH.add(eng, fn, reads, writes)

    def stt(eng, out, in0, scalar, in1, op0, op1, reads, writes):
        SCH.add(eng, lambda e, a=(out, in0, scalar, in1, op0, op1): e.scalar_tensor_tensor(*a), reads, writes)

    def tt(eng, out, in0, in1, op, reads, writes):
        SCH.add(eng, lambda e, a=(out, in0, in1, op): e.tensor_tensor(*a), reads, writes)

    def cp(eng, out, in_, reads, writes):
        if eng == "act":
            SCH.add(eng, lambda e, a=(out, in_): e.copy(*a), reads, writes)
        else:
            SCH.add(eng, lambda e, a=(out, in_): e.tensor_copy(*a), reads, writes)

    def memset(eng, ap, val, reads, writes):
        SCH.add(eng, lambda e, a=(ap, val): e.memset(*a), reads, writes)

    def recip(out, in_, reads, writes):
        SCH.add("dve", lambda e, a=(out, in_): e.reciprocal(*a), reads, writes)

    lam_init = [0.8 - 0.6 * math.exp(-0.3 * l) for l in range(2)]

    AR.reset()
    dma("sp", [(CST[:, :], cst_d[:, :])], [], ["CST"], "cst")
    cp("dve", CBF[:, :], CST[:, :], ["CST"], ["CBF"])
    PIN = AR.alloc([7, 128], F32)
    dma("sp", [(PIN[:, b, :], prm_d[b * 128:(b + 1) * 128, :]) for b in range(7)], [], ["PIN"], "pin")
    for b in range(7):
        bank = PS[b % 2]
        transp(bank[:, 0:128], PIN[:, b, :], ["PIN", "CST"], [("ps", b % 2)])
        cp("dve", PT[:, b * 128:(b + 1) * 128], bank[:, 0:128], [("ps", b % 2)], ["PT"])
    dma("sp", [(SG[:, :], sub_d[0, :].partition_broadcast(128))], [], ["SG"], "sg")
    ts("dve", GS[:, :], PT[:, 0:144], float(math.sqrt(D)), None, ALU.mult, None, ["PT"], ["GS"])
    for l in range(2):
        ts("dve", SG[:, l * 256:(l + 1) * 256], SG[:, l * 256:(l + 1) * 256], float(16.0 * (1.0 - lam_init[l])), None,
           ALU.mult, None, ["SG"], ["SG"])
    for l in range(2):
        ts("dve", QKG[:, l:l + 1], PT[:, P_QG + l:P_QG + l + 1], float(math.sqrt(128.0) * SCALE), None, ALU.mult, None,
           ["PT"], ["QKG"])
        ts("dve", QKG[:, 2 + l:3 + l], PT[:, P_KG + l:P_KG + l + 1], float(math.sqrt(128.0)), None, ALU.mult, None,
           ["PT"], ["QKG"])
    for l in range(2):
        for i in range(2):
            a = P_LAM + 4 * l + 2 * i
            tt("dve", LAM[:, 2 * l + i:2 * l + i + 1], PT[:, a:a + 1], PT[:, a + 1:a + 2], ALU.mult, ["PT"], ["LAM"])
    mm(PS[2][:, 0:4], [(ones_f, LAM[:, 0:4])], ["LAM", "CST"], [("ps", 2)])
    act(LAM[:, 4:8], PS[2][:, 0:4], AF.Exp, [("ps", 2)], ["LAM"])
    for l in range(2):
        tt("dve", LAM[:, 8 + l:9 + l], LAM[:, 4 + 2 * l:5 + 2 * l], LAM[:, 5 + 2 * l:6 + 2 * l], ALU.subtract,
           ["LAM"], ["LAM"])
        ts("dve", LAM[:, 10 + l:11 + l], LAM[:, 8 + l:9 + l], float(lam_init[l]), -1.0, ALU.add, ALU.mult,
           ["LAM"], ["LAM"])
    PI = math.pi
    for s in range(NSEQ):
        POSI = AR.alloc([S], I32)
        ANG = AR.alloc([S], F32)
        U = AR.alloc([2, S], F32)
        dma("sp", [(POSI[0:32, :], pos_d[s, :].partition_broadcast(32))], [], [("POSI", s)], "posi")
        cp("dve", ANG[0:32, :], POSI[0:32, :], [("POSI", s)], [("ANG", s)])
        ts("dve", ANG[0:32, :], ANG[0:32, :], CST[0:32, C_INVF:C_INVF + 1], None, ALU.mult, None,
           [("ANG", s), "CST"], [("ANG", s)])
        KI = AR.alloc([S], I32)
        KF = AR.alloc([S], F32)
        XS = AR.alloc([S], F32)
        MM = AR.alloc([S], F32)
        C1 = 6.28125
        C2 = 2 * PI - C1
        for i, sh in enumerate((0.5 * PI, 0.0)):
            ts("dve", XS[0:32, :], ANG[0:32, :], float(sh), None, ALU.add, None, [("ANG", s)], [("XS", s)])
            ts("dve", KI[0:32, :], XS[0:32, :], float(1.0 / (2 * PI)), None, ALU.mult, None, [("XS", s)], [("KI", s)])
            cp("dve", KF[0:32, :], KI[0:32, :], [("KI", s)], [("KF", s)])
            stt("dve", U[0:32, i, :], KF[0:32, :], float(-C1), XS[0:32, :], ALU.mult, ALU.add,
                [("KF", s), ("XS", s)], [("U", s, i)])
            stt("dve", U[0:32, i, :], KF[0:32, :], float(-C2), U[0:32, i, :], ALU.mult, ALU.add,
                [("KF", s), ("U", s, i)], [("U", s, i)])
            ts("dve", MM[0:32, :], U[0:32, i, :], float(PI), float(2 * PI), ALU.is_gt, ALU.mult, [("U", s, i)], [("MM", s)])
            tt("dve", U[0:32, i, :], U[0:32, i, :], MM[0:32, :], ALU.subtract, [("U", s, i), ("MM", s)], [("U", s, i)])
            ts("dve", U[0:32, i, :], U[0:32, i, :], 3.14159, -3.14159, ALU.min, ALU.max, [("U", s, i)], [("U", s, i)])
            act(U[0:32, i, :], U[0:32, i, :], AF.Sin, [("U", s, i)], [("U", s, i)])
        dma("sp", [(cs_d[s, i, :, :], U[0:32, i, :]) for i in range(2)], [("U", s, 0), ("U", s, 1)], [("cs", s)], "csst")
    SCH.barrier()

    wctr = [0]

    def run_jobs(jobs, W):
        NW = len(W)
        wj = [j for j in jobs]
        slots = []
        n = len(wj)
        PREF = NW - 1
        for i in range(n + PREF):
            j = i - PREF
            if j >= 0:
                wj[j][2](slots[j])
            if i < n:
                src, nk, _ = wj[i]
                sl = wctr[0] % NW
                wctr[0] += 1
                pairs = []
                for k0 in range(0, nk, 4):
                    k1 = min(nk, k0 + 4)
                    pairs.append((W[sl][:, k0:k1, :],
                                  src[k0 * 128:k1 * 128, :].rearrange("(k p) n -> p k n", p=128)))
                dma("pool", pairs, [], [("W", sl)], "W%d" % sl)
                slots.append(sl)

    def load_ht(HT, s, t0, q="sp"):
        pairs = [(HT[:, c0:c0 + 4, :], hT_d[s, c0:c0 + 4, :, t0:t0 + TT].rearrange("c p t -> p c t"))
                 for c0 in range(0, NCH, 4)]
        dma(q, pairs, [("hT", s, t0)], ["HT"], "HT")

    def store_ht(HT, s, t0, q="sp"):
        pairs = [(hT_d[s, c0:c0 + 4, :, t0:t0 + TT].rearrange("c p t -> p c t"), HT[:, c0:c0 + 4, :])
                 for c0 in range(0, NCH, 4)]
        dma(q, pairs, ["HT"], [("hT", s, t0)], "HTst")

    def rms_stats(HT, SQ, RS, psb):
        for c in range(NCH):
            sq = SQ[:, c % 2, :]
            act(sq, HT[:, c, :], AF.Square, ["HT"], [("SQ", c % 2)])
            mm(PS[psb][:, :], [(ones_b, sq)], [("SQ", c % 2), "CBF"], [("ps", psb)], start=(c == 0), stop=(c == NCH - 1))
        act(RS[:, :], PS[psb][:, :], AF.Ln, [("ps", psb)], ["RS"], bias=float(EPS * D))
        act(RS[:, :], RS[:, :], AF.Exp, ["RS"], ["RS"], scale=-0.5)

    def rms_apply(XN, xname, HT, RS, gbase):
        for c in range(NCH):
            stt("dve", XN[:, c, :], HT[:, c, :], GS[:, gbase + c:gbase + c + 1], RS[:, :], ALU.mult, ALU.mult,
                ["HT", "RS", "GS"], [xname])

    def phase_proj(l):
        AR.reset()
        HT = AR.alloc([NCH, TT], F32)
        XN = AR.alloc([NCH, TT], BF16)
        XN2 = AR.alloc([NCH, TT], BF16) if l == 2 else None
        W = [AR.alloc([NCH, 512], BF16) for _ in range(4)]
        SQ = AR.alloc([2, TT], BF16)
        RS = AR.alloc([TT], F32)
        RS2 = AR.alloc([2, TT], F32)
        QN = AR.alloc([4, TT], BF16)
        T1 = AR.alloc([2, TT], F32)
        T2 = AR.alloc([2, TT], F32)
        CS = AR.alloc([2, TT], F32)
        VST = AR.alloc([2, NQ, 512], BF16)
        XIN = AR.alloc([2, D], F32) if l == 0 else None
        ctr = {"qn": 0, "vst": 0, "pp": 0}
        jobs = []
        pipe = []

        def flush_pipe():
            while pipe:
                for st in list(pipe):
                    st.pop(0)()
                    if not st:
                        pipe.remove(st)

        for s in range(NSEQ):
            for ti in range(NTT):
                t0 = ti * TT

                def prologue(s=s, t0=t0):
                    flush_pipe()
                    if l == 0:
                        for tq in range(NQ):
                            xi = tq % 2
                            dma("sp", [(XIN[:, xi, :], x_d[s, t0 + tq * 128:t0 + (tq + 1) * 128, :])], [],
                                [("XIN", xi)], "XIN%d" % xi)
                            for b in range(4):
                                pb = b % 2
                                for j in range(4):
                                    c = b * 4 + j
                                    transp(PS[pb][:, j * 128:(j + 1) * 128], XIN[:, xi, c * 128:(c + 1) * 128],
                                           [("XIN", xi), "CST"], [("ps", pb)])
                                cp("act" if b % 2 else "dve", HT[:, b * 4:(b + 1) * 4, tq * 128:(tq + 1) * 128],
                                   PS[pb][:, :].rearrange("p (a b) -> p a b", a=4), [("ps", pb)], ["HT"])
                        store_ht(HT, s, t0)
                    else:
                        load_ht(HT, s, t0)
                    rms_stats(HT, SQ, RS, 4)
                    rms_apply(XN, "XN", HT, RS, P_ATTN + l * 16)
                    if l == 2:
                        rms_apply(XN2, "XN2", HT, RS, P_KV)
                    if l < 2:
                        dma("sp", [(CS[0:32, i, :], cs_d[s, i, :, t0:t0 + TT]) for i in range(2)], [("cs", s)], ["CS"], "CS")

                def qk_group(sl, cg, wsel, s=s, t0=t0, first=False, prologue=prologue):
                    if first:
                        prologue()
                    Wv = W[sl]
                    xn = XN2 if wsel == "bk" else XN
                    xname = "XN2" if wsel == "bk" else "XN"
                    for j in range(4):
                        hc = cg * 4 + j
                        pq = ctr["pp"] % 4
                        pb = ctr["pp"] % 2
                        ctr["pp"] += 1
                        mm(PS[pq][:, :], [(Wv[:, k, j * 128:(j + 1) * 128], xn[:, k, :]) for k in range(NCH)],
                           [("W", sl), xname], [("ps", pq)])
                        qi = ctr["qn"] % 4
                        ctr["qn"] += 1
                        qn = QN[:, qi, :]
                        dst = kT_d if wsel in ("ak", "bk") else qT_d
                        if wsel in ("aq", "ak"):
                            sq = SQ[:, pb, :]
                            act(sq, PS[pq][:, :], AF.Square, [("ps", pq)], [("SQ", pb)])
                            gcol = (0 if wsel == "aq" else 2) + l

                            def stage_b(pq=pq, pb=pb, sq=sq, qn=qn, qi=qi, gcol=gcol):
                                mm(PS[4 + pb][:, :], [(ones_b, sq)], [("SQ", pb), "CBF"], [("ps", 4 + pb)])
                                act(RS2[:, pb, :], PS[4 + pb][:, :], AF.Ln, [("ps", 4 + pb)], [("RS2", pb)], bias=float(EPS * 128))
                                act(RS2[:, pb, :], RS2[:, pb, :], AF.Exp, [("RS2", pb)], [("RS2", pb)], scale=-0.5)
                                stt("dve", qn, PS[pq][:, :], QKG[:, gcol:gcol + 1], RS2[:, pb, :], ALU.mult, ALU.mult,
                                    [("ps", pq), ("RS2", pb), "QKG"], [("QN", qi)])

                            def stage_c(pb=pb, qn=qn, qi=qi, dst=dst, s=s, hc=hc, t0=t0, wsel=wsel):
                                mm(PS[6 + pb][0:32, :], [(rrot_b, qn[0:32, :])], [("QN", qi), "CBF"], [("ps", 6 + pb)])
                                tt("dve", T1[0:32, pb, :], PS[6 + pb][0:32, :], CS[0:32, 1, :], ALU.mult,
                                   [("ps", 6 + pb), "CS"], [("T1", pb)])
                                tt("pool", T2[0:32, pb, :], qn[0:32, :], CS[0:32, 0, :], ALU.mult,
                                   [("QN", qi), "CS"], [("T2", pb)])
                                tt("pool", qn[0:32, :], T1[0:32, pb, :], T2[0:32, pb, :], ALU.add,
                                   [("T1", pb), ("T2", pb)], [("QN", qi)])
                                dma("sp", [(dst[s, hc, :, t0:t0 + TT], qn)], [("QN", qi)], [("qk", wsel, s, hc, t0)], "QN%d" % qi)

                            for st in list(pipe):
                                st.pop(0)()
                                if not st:
                                    pipe.remove(st)
                            pipe.append([stage_b, stage_c])
                            continue
                        elif wsel == "bq":
                            SCH.add("act", lambda e, a=(qn, PS[pq][:, :], float(SCALE)): e.mul(*a), [("ps", pq)], [("QN", qi)])
                        else:
                            cp("act", qn, PS[pq][:, :], [("ps", pq)], [("QN", qi)])
                        dma("sp", [(dst[s, hc, :, t0:t0 + TT], qn)], [("QN", qi)], [("qk", wsel, s, hc, t0)], "QN%d" % qi)

                def v_group(sl, cg, xsel, s=s, t0=t0):
                    flush_pipe()
                    Wv = W[sl]
                    xn = XN2 if xsel == "XN2" else XN
                    vi = ctr["vst"] % 2
                    ctr["vst"] += 1
                    for tq in range(NQ):
                        pb = ctr["pp"] % 4
                        ctr["pp"] += 1
                        mm(PS[pb][:, :], [(xn[:, k, tq * 128:(tq + 1) * 128], Wv[:, k, :]) for k in range(NCH)],
                           [("W", sl), xsel], [("ps", pb)])
                        cp("act", VST[:, vi, tq, :], PS[pb][:, :], [("ps", pb)], [("VST", vi)])
                    dma("sp", [(v_d[s, t0:t0 + TT, cg * 512:(cg + 1) * 512].rearrange("(q p) e -> p q e", p=128),
                                VST[:, vi, :, :])], [("VST", vi)], [("v", s, t0, cg)], "VST%d" % vi)

                if l < 2:
                    for cg in range(8):
                        wsel = "aq" if cg < 4 else "ak"
                        jobs.append((wqkv_d[l, :, cg * 512:(cg + 1) * 512], NCH,
                                     lambda sl, cg=cg, wsel=wsel, f=qk_group, first=(cg == 0): f(sl, cg % 4, wsel, first=first)))
                    for cg in range(4):
                        jobs.append((wqkv_d[l, :, (8 + cg) * 512:(9 + cg) * 512], NCH,
                                     lambda sl, cg=cg, f=v_group: f(sl, cg, "XN")))
                else:
                    jb = l - 2
                    for cg in range(4):
                        jobs.append((bwq_d[jb, :, cg * 512:(cg + 1) * 512], NCH,
                                     lambda sl, cg=cg, f=qk_group, first=(cg == 0): f(sl, cg, "bq", first=first)))
                    if l == 2:
                        for cg in range(4):
                            jobs.append((bkv_d[:, cg * 512:(cg + 1) * 512], NCH,
                                         lambda sl, cg=cg, f=qk_group: f(sl, cg, "bk")))
                        for cg in range(4):
                            jobs.append((bkv_d[:, (4 + cg) * 512:(5 + cg) * 512], NCH,
                                         lambda sl, cg=cg, f=v_group: f(sl, cg, "XN2")))
        run_jobs(jobs, W)
        flush_pipe()
        SCH.barrier()

    def tri_off(kt):
        return kt * S - 128 * (kt * (kt - 1) // 2)

    TRI = tri_off(NT)

    def chunks(lo, hi):
        c = []
        while lo < hi:
            n = min(512, hi - lo)
            c.append((lo, n))
            lo += n
        return c

    def phase_attn_a(l):
        AR.reset()
        QK = [AR.alloc([4, S], BF16) for _ in range(2)]
        VE = [AR.alloc([NT, 258], BF16) for _ in range(2)]
        PTS = [AR.alloc([TRI], BF16) for _ in range(2)]
        RR = AR.alloc([4, 4], F32)
        T1 = AR.alloc([4, 256], F32)
        OO = AR.alloc([4, 256], F32)
        OF = AR.alloc([4, 256], F32)
        OS = AR.alloc([4, 2, 258], F32)
        JK = AR.alloc([256], F32)
        OTS = AR.alloc([2, 2, TT], BF16)
        for i in range(2):
            memset("pool", VE[i][:, :, 256:258], 1.0, [], [("VE", i)])
        hi = 0
        pp = 0
        fin = 0
        tails = []
        for s in range(NSEQ):
            for h in range(8):
                b = hi % 2
                hi += 1
                dma("sp", [(QK[b][:, 0, :], qT_d[s, 2 * h, :, :]), (QK[b][:, 1, :], qT_d[s, 2 * h + 1, :, :]),
                           (QK[b][:, 2, :], kT_d[s, 2 * h, :, :]), (QK[b][:, 3, :], kT_d[s, 2 * h + 1, :, :])],
                    [], [("QK", b)], "QK%d" % b)
                vp = []
                for k0 in range(0, NT, 4):
                    k1 = min(NT, k0 + 4)
                    vp.append((VE[b][:, k0:k1, 0:256],
                               v_d[s, k0 * 128:k1 * 128, h * 256:(h + 1) * 256].rearrange("(t p) e -> p t e", p=128)))
                dma("sp", vp, [], [("VE", b)], "VE%d" % b)
                for c in range(2):
                    for kt in range(NT):
                        for (q0, n) in chunks(kt * 128, S):
                            pb = pp % 2
                            pp += 1
                            mm(PS[pb][:, 0:n], [(QK[b][:, 2 + c, kt * 128:(kt + 1) * 128], QK[b][:, c, q0:q0 + n])],
                               [("QK", b)], [("ps", pb)])
                            o = tri_off(kt) + (q0 - kt * 128)
                            act(PTS[c][:, o:o + n], PS[pb][:, 0:n], AF.Exp, [("ps", pb)], [("PTS", c, kt)])
                        o = tri_off(kt)
                        memset("pool", PTS[c][64:128, o:o + 64], 0.0, [], [("PTS", c, kt)])
                for qt in range(NT):
                    pf = fin % 2
                    f = fin % 4
                    fin += 1
                    for c in range(2):
                        pb = 2 + 2 * pf + c
                        mm(PS[pb][:, 0:257],
                           [(PTS[c][:, tri_off(kt) + (qt - kt) * 128: tri_off(kt) + (qt - kt + 1) * 128],
                             VE[b][:, kt, 0:257]) for kt in range(qt + 1)],
                           [("PTS", c, kt) for kt in range(qt + 1)] + [("VE", b)], [("ps", pb)])
                        cp("dve", OS[:, f, c, 0:257], PS[pb][:, 0:257], [("ps", pb)], [("OS", f, c)])
                    p0, p1 = OS[:, f, 0, :], OS[:, f, 1, :]
                    recip(RR[:, f, 0:1], p0[:, 256:257], [("OS", f, 0)], [("RR", f)])
                    recip(RR[:, f, 1:2], p1[:, 256:257], [("OS", f, 1)], [("RR", f)])
                    tt("dve", RR[:, f, 2:3], RR[:, f, 1:2], LAM[:, 10 + l:11 + l], ALU.mult, [("RR", f), "LAM"], [("RR", f)])
                    ts("dve", T1[:, f, :], p1[:, 0:256], RR[:, f, 2:3], None, ALU.mult, None,
                       [("OS", f, 1), ("RR", f)], [("T1", f)])
                    stt("dve", OO[:, f, :], p0[:, 0:256], RR[:, f, 0:1], T1[:, f, :], ALU.mult, ALU.add,
                        [("OS", f, 0), ("RR", f), ("T1", f)], [("OO", f)])
                    memset("pool", RR[:, f, 3:4], 0.0, [], [("RS3", f)])
                    act(JK[:, :], OO[:, f, :], AF.Square, [("OO", f), ("RS3", f)], ["JK", ("RS3", f)],
                        accum_out=RR[:, f, 3:4])
                    act(RR[:, f, 3:4], RR[:, f, 3:4], AF.Ln, [("RS3", f)], [("RS3", f)], bias=float(EPS * 256))
                    act(RR[:, f, 3:4], RR[:, f, 3:4], AF.Exp, [("RS3", f)], [("RS3", f)], scale=-0.5)
                    stt("dve", OF[:, f, :], OO[:, f, :], RR[:, f, 3:4], SG[:, l * 256:(l + 1) * 256], ALU.mult, ALU.mult,
                        [("OO", f), ("RS3", f), "SG"], [("OF", f)])
                    def tail(qt=qt, f=f, pf=pf, s=s, h=h):
                        tb = 6 + pf
                        for i in range(2):
                            transp(PS[tb][:, i * 128:(i + 1) * 128], OF[:, f, i * 128:(i + 1) * 128], [("OF", f), "CST"],
                                   [("ps", tb)])
                        g = qt // NQ
                        ob = g % 2
                        cp("act", OTS[:, ob, :, (qt % NQ) * 128:(qt % NQ + 1) * 128],
                           PS[tb][:, 0:256].rearrange("p (a b) -> p a b", a=2), [("ps", tb)], [("OTS", ob)])
                        if qt % NQ == NQ - 1:
                            dma("sp", [(oT_d[s, 2 * h:2 * h + 2, :, g * TT:(g + 1) * TT].rearrange("c p t -> p c t"),
                                        OTS[:, ob, :, :])], [("OTS", ob)], [("oT", s, h, g)], "OTS%d" % ob)
                    tails.append(tail)
                    if len(tails) > 2:
                        tails.pop(0)()
        while tails:
            tails.pop(0)()
        SCH.barrier()

    def phase_attn_b():
        AR.reset()
        QB = [AR.alloc([2, S], BF16) for _ in range(2)]
        VB = [AR.alloc([NT, 128], BF16) for _ in range(2)]
        SP = AR.alloc([TRI], BF16)
        SS = AR.alloc([TRI], BF16)
        ACC = AR.alloc([S], F32)
        EE = AR.alloc([2, 512], F32)
        AT = AR.alloc([3, 512], BF16)
        OTS = AR.alloc([2, TT], BF16)
        hi = 0
        pp = 0
        lp = 0
        ai = 0
        oi = 0
        for s in range(NSEQ):
            for h in range(16):
                b = hi % 2
                hi += 1
                dma("sp", [(QB[b][:, 0, :], qT_d[s, h, :, :]), (QB[b][:, 1, :], kT_d[s, h, :, :])],
                    [], [("QB", b)], "QB%d" % b)
                vp = []
                for k0 in range(0, NT, 4):
                    k1 = min(NT, k0 + 4)
                    vp.append((VB[b][:, k0:k1, :],
                               v_d[s, k0 * 128:k1 * 128, h * 128:(h + 1) * 128].rearrange("(t p) e -> p t e", p=128)))
                dma("sp", vp, [], [("VB", b)], "VB%d" % b)
                for kt in range(NT):
                    for (q0, n) in chunks(kt * 128, S):
                        pb = pp % 2
                        pp += 1
                        mm(PS[pb][:, 0:n], [(QB[b][:, 1, kt * 128:(kt + 1) * 128], QB[b][:, 0, q0:q0 + n])],
                           [("QB", b)], [("ps", pb)])
                        act(EE[:, pb, 0:n], PS[pb][:, 0:n], AF.Exp, [("ps", pb)], [("EE", pb)])
                        o = tri_off(kt) + (q0 - kt * 128)
                        act(SP[:, o:o + n], EE[:, pb, 0:n], AF.Ln, [("EE", pb)], [("SP", kt)], bias=1.0)
                    o = tri_off(kt)
                    tt("dve", SP[:, o:o + 128], SP[:, o:o + 128], maskb_b, ALU.mult, [("SP", kt), "CBF"], [("SP", kt)])
                memset("dve", ACC[:, :], 0.0, [], ["ACC"])
                for kt in range(NT - 2, -1, -1):
                    lo = (kt + 1) * 128
                    o1 = tri_off(kt + 1)
                    tt("dve", ACC[:, lo:S], ACC[:, lo:S], SP[:, o1:o1 + (S - lo)], ALU.add, ["ACC", ("SP", kt + 1)], ["ACC"])
                    o = tri_off(kt)
                    cp("dve", SS[:, o + 128:o + 128 + (S - lo)], ACC[:, lo:S], ["ACC"], [("SS", kt)])
                for Q in range(S // 512):
                    ob = 4 + (oi % 2)
                    oi += 1
                    ktmax = min(NT - 1, 4 * Q + 3)
                    pend_av = None
                    for kt in range(ktmax + 1):
                        q0 = max(Q * 512, kt * 128)
                        n = (Q + 1) * 512 - q0
                        lo = q0 - Q * 512
                        lb = 2 + (lp % 2)
                        lp += 1
                        o = tri_off(kt) + (q0 - kt * 128)
                        pairs = [(QB[b][:, 1, kt * 128:(kt + 1) * 128], QB[b][:, 0, q0:q0 + n]),
                                 (ntri_b, SP[:, o:o + n])]
                        rds = [("QB", b), ("SP", kt), "CBF"]
                        q1 = max(q0, (kt + 1) * 128)
                        has_ss = q1 < (Q + 1) * 512 and kt < NT - 1
                        mm(PS[lb][:, lo:512], pairs, rds, [("ps", lb)], start=True, stop=not has_ss)
                        if has_ss:
                            o2 = tri_off(kt) + (q1 - kt * 128)
                            mm(PS[lb][:, q1 - Q * 512:512], [(nones_b, SS[:, o2:o2 + (Q + 1) * 512 - q1])],
                               [("SS", kt), "CBF"], [("ps", lb)], start=False, stop=True)
                        if pend_av is not None:
                            pend_av()
                        a = ai % 3
                        ai += 1
                        act(AT[:, a, 0:n], PS[lb][:, lo:512], AF.Exp, [("ps", lb)], [("AT", a)])
                        if kt * 128 >= Q * 512:
                            tt("dve", AT[:, a, 0:128], AT[:, a, 0:128], maskb_b, ALU.mult, [("AT", a), "CBF"], [("AT", a)])

                        def pend_av(kt=kt, a=a, n=n, lo=lo, ob=ob, b=b, ktmax=ktmax):
                            mm(PS[ob][:, lo:512], [(VB[b][:, kt, :], AT[:, a, 0:n])], [("VB", b), ("AT", a)], [("ps", ob)],
                               start=(kt == 0), stop=(kt == ktmax))
                    pend_av()
                    sb_ = oi % 2
                    cp("act", OTS[:, sb_, :], PS[ob][:, :], [("ps", ob)], [("OTS", sb_)])
                    dma("sp", [(oT_d[s, h, :, Q * 512:(Q + 1) * 512], OTS[:, sb_, :])], [("OTS", sb_)], [("oT", s, h, Q)],
                        "OTS%d" % sb_)
        SCH.barrier()

    def phase_ffn(l, last):
        AR.reset()
        HT = AR.alloc([NCH, TT], F32)
        XN = AR.alloc([NCH, TT], BF16)
        W = [AR.alloc([NCH, 512], BF16) for _ in range(3)]
        ACTB = AR.alloc([NFF, TT], BF16)
        SQ = AR.alloc([2, TT], BF16)
        RS = AR.alloc([TT], F32)
        GE = AR.alloc([2, TT + 2], F32)
        A1 = AR.alloc([2, TT], F32)
        A2 = AR.alloc([2, TT], F32)
        UU = AR.alloc([2, TT], F32)
        XO = AR.alloc([D], F32) if last else None
        wo_src = awo_d[l] if l < 2 else bwo_d[l - 2]
        cwb = P_CW + l * 3 * NFF
        cbb = P_CB + l * NFF
        ctr = {"pp": 0, "ge": 0}
        jobs = []
        for s in range(NSEQ):
            for ti in range(NTT):
                t0 = ti * TT

                def prologue(s=s, t0=t0):
                    load_ht(HT, s, t0)
                    dma("sp", [(XN[:, c0:c0 + 4, :], oT_d[s, c0:c0 + 4, :, t0:t0 + TT].rearrange("c p t -> p c t"))
                               for c0 in range(0, NCH, 4)], [], ["XN"], "XN")

                def oproj(sl, cg, first, s=s, t0=t0, ti=ti, prologue=prologue):
                    if first:
                        prologue()
                    for j in range(4):
                        oc = cg * 4 + j
                        pb = ctr["pp"] % 2
                        ctr["pp"] += 1
                        mm(PS[pb][:, :], [(W[sl][:, k, j * 128:(j + 1) * 128], XN[:, k, :]) for k in range(NCH)],
                           [("W", sl), "XN"], [("ps", pb)])
                        tt("dve", HT[:, oc, :], HT[:, oc, :], PS[pb][:, :], ALU.add, ["HT", ("ps", pb)], ["HT"])
                    if cg == 3:
                        rms_stats(HT, SQ, RS, 2)
                        rms_apply(XN, "XN", HT, RS, P_FFN + l * 16)

                def up(slu, slg, gi, s=s, t0=t0, ti=ti):
                    for j in range(4):
                        c = gi * 4 + j
                        pb = ctr["pp"] % 2
                        ctr["pp"] += 1
                        pu, pg = PS[pb], PS[2 + pb]
                        mm(pu[:, :], [(W[slu][:, k, j * 128:(j + 1) * 128], XN[:, k, :]) for k in range(NCH)],
                           [("W", slu), "XN"], [("ps", pb)])
                        mm(pg[:, :], [(W[slg][:, k, j * 128:(j + 1) * 128], XN[:, k, :]) for k in range(NCH)],
                           [("W", slg), "XN"], [("ps", 2 + pb)])
                        gi_ = ctr["ge"] % 2
                        ctr["ge"] += 1
                        ge = GE[:, gi_, :]
                        cp("act", UU[:, gi_, :], pu[:, :], [("ps", pb)], [("UU", gi_)])
                        if ti == 0:
                            memset("pool", ge[:, 0:2], 0.0, [], [("GE", gi_)])
                        else:
                            cp("pool", ge[:, 0:2], HALO[:, c, :], [("HALO", c)], [("GE", gi_)])
                        cp("act", ge[:, 2:TT + 2], pg[:, :], [("ps", 2 + pb)], [("GE", gi_)])
                        cp("pool", HALO[:, c, :], ge[:, TT:TT + 2], [("GE", gi_)], [("HALO", c)])
                        w0 = PT[:, cwb + 0 * NFF + c:cwb + 0 * NFF + c + 1]
                        w1 = PT[:, cwb + 1 * NFF + c:cwb + 1 * NFF + c + 1]
                        w2 = PT[:, cwb + 2 * NFF + c:cwb + 2 * NFF + c + 1]
                        bb = PT[:, cbb + c:cbb + c + 1]
                        ts("dve", A1[:, gi_, :], ge[:, 0:TT], w0, bb, ALU.mult, ALU.add, [("GE", gi_), "PT"], [("A1", gi_)])
                        stt("dve", A2[:, gi_, :], ge[:, 1:TT + 1], w1, A1[:, gi_, :], ALU.mult, ALU.add,
                            [("GE", gi_), ("A1", gi_), "PT"], [("A2", gi_)])
                        stt("dve", A1[:, gi_, :], ge[:, 2:TT + 2], w2, A2[:, gi_, :], ALU.mult, ALU.add,
                            [("GE", gi_), ("A2", gi_), "PT"], [("A1", gi_)])
                        act(A2[:, gi_, :], A1[:, gi_, :], AF.Silu, [("A1", gi_)], [("A2", gi_)])
                        tt("dve", ACTB[:, c, :], A2[:, gi_, :], UU[:, gi_, :], ALU.mult, [("A2", gi_), ("UU", gi_)], [("ACTB", c)])

                def down(sl, cg, kp, s=s, t0=t0, ti=ti):
                    for j in range(4):
                        pd = 4 + j
                        mm(PS[pd][:, :], [(W[sl][:, kk, j * 128:(j + 1) * 128], ACTB[:, kp * 11 + kk, :]) for kk in range(11)],
                           [("W", sl)] + [("ACTB", kp * 11 + kk) for kk in range(11)], [("ps", pd)],
                           start=(kp == 0), stop=(kp == 3))
                    if kp == 3:
                        for j in range(4):
                            oc = cg * 4 + j
                            tt("dve", HT[:, oc, :], HT[:, oc, :], PS[4 + j][:, :], ALU.add, ["HT", ("ps", 4 + j)], ["HT"])
                        if cg == 3:
                            if not last:
                                store_ht(HT, s, t0)
                            else:
                                for tq in range(NQ):
                                    for b4 in range(4):
                                        pb = b4 % 2
                                        for j in range(4):
                                            c = b4 * 4 + j
                                            transp(PS[pb][:, j * 128:(j + 1) * 128], HT[:, c, tq * 128:(tq + 1) * 128],
                                                   ["HT", "CST"], [("ps", pb)])
                                        cp("act" if b4 % 2 else "dve", XO[:, b4 * 512:(b4 + 1) * 512], PS[pb][:, :],
                                           [("ps", pb)], ["XO"])
                                    dma("sp", [(out_d[s, t0 + tq * 128:t0 + (tq + 1) * 128, :], XO[:, :])], ["XO"],
                                        [("out", s, t0, tq)], "XO")

                for cg in range(4):
                    jobs.append((wo_src[:, cg * 512:(cg + 1) * 512], NCH,
                                 lambda sl, cg=cg, f=oproj: f(sl, cg, cg == 0)))
                for gi in range(11):
                    holder = {}
                    jobs.append((wup_d[l, :, gi * 512:(gi + 1) * 512], NCH,
                                 lambda sl, holder=holder: holder.__setitem__("u", sl)))
                    jobs.append((wup_d[l, :, DFF + gi * 512:DFF + (gi + 1) * 512], NCH,
                                 lambda sl, gi=gi, holder=holder, f=up: f(holder["u"], sl, gi)))
                for cg in range(4):
                    for kp in range(4):
                        jobs.append((wdn_d[l, kp * 1408:(kp + 1) * 1408, cg * 512:(cg + 1) * 512], 11,
                                     lambda sl, cg=cg, kp=kp, f=down: f(sl, cg, kp)))
        run_jobs(jobs, W)
        SCH.barrier()

    for l in range(n_layers):
        phase_proj(l)
        if l < 2:
            phase_attn_a(l)
        else:
            phase_attn_b()
        phase_ffn(l, last=(l == n_layers - 1))

    with ExitStack() as stack:
        SCH.emit(nc, stack)
    return nc, SCH


W_NAMES = ["a_w_qkv", "a_w_o", "b_w_kv", "b_w_q", "b_w_o", "ffn_w_up", "ffn_w_down"]


def make_in_maps(inputs, n_cores, nseq, S):
    f = lambda a: np.ascontiguousarray(np.asarray(a), dtype=np.float32)
    cst = make_consts()
    prm = pack_params(f(inputs["attn_norm_g"]), f(inputs["ffn_norm_g"]), f(inputs["kv_norm_g"]),
                      f(inputs["ffn_conv_w"]), f(inputs["ffn_conv_b"]), f(inputs["a_q_norm_g"]),
                      f(inputs["a_k_norm_g"]), f(inputs["a_lambda_q1"]), f(inputs["a_lambda_k1"]),
                      f(inputs["a_lambda_q2"]), f(inputs["a_lambda_k2"]))
    sub = f(inputs["a_subln_g"]).reshape(1, 512)
    ws = {k: f(inputs[k]) for k in W_NAMES}
    x = np.asarray(inputs["x"])
    pos = np.asarray(inputs["positions"]).astype(np.int32)
    maps = []
    for c in range(n_cores):
        m = {"x": np.ascontiguousarray(x[c * nseq:(c + 1) * nseq, :S], dtype=np.float32),
             "pos": np.ascontiguousarray(pos[c * nseq:(c + 1) * nseq, :S]),
             "cst": cst, "prm": prm, "subln": sub}
        m.update(ws)
        maps.append(m)
    return maps


def kernel(**inputs):
    n_cores, nseq, S = 8, 2, 2048
    nc, _ = build(S=S, NSEQ=nseq)
    maps = make_in_maps(inputs, n_cores, nseq, S)
    res = run_bass_kernel_spmd(nc, maps, core_ids=list(range(n_cores)))
    return np.concatenate([np.asarray(r["out"], dtype=np.float32) for r in res.results], axis=0)
```
